# Optimizing a Trainium2 kernel written in Bass

```python
import math
import numpy as np
import jax
import jax.numpy as jnp
from jax import lax

D_MODEL = 1024
BATCH = 2
SEQ = 8192
DEPTH = 2

DN_HEADS = 4
DN_HEAD_DIM = 128
DN_WIDTH = DN_HEADS * DN_HEAD_DIM
DN_CONV = 4
DN_CHUNK = 64
POOL_GROUPS = 4
POOL_GROUP_DIM = 128
POOL_WIDTH = POOL_GROUPS * POOL_GROUP_DIM
POOL_WINDOWS = (2, 4, 8, 16)
SG_GROUPS = 4
SG_GROUP_DIM = 128
SG_WIDTH = SG_GROUPS * SG_GROUP_DIM
SG_CHUNK = 128
RET_HEADS = 4
RET_HEAD_DIM = 128
RET_WIDTH = RET_HEADS * RET_HEAD_DIM
RET_CHUNK = 128
ROPE_BASE = 10000.0
N_BRANCH = 4
D_FF = 4 * D_MODEL
N_MOD = 6
EPS = 1e-6
SPLIT_SIZES = (DN_WIDTH, DN_WIDTH, DN_WIDTH, DN_WIDTH, DN_HEADS, DN_HEADS,
               POOL_WIDTH, SG_WIDTH, SG_WIDTH,
               RET_WIDTH, RET_WIDTH, RET_WIDTH, RET_WIDTH,
               N_BRANCH * D_MODEL)
IN_COLS = sum(SPLIT_SIZES)

kernel_name = 'hybrid_parallel_mixer_trunk'


def rmsnorm(x, g):
    xf = x.astype(jnp.float32)
    y = xf * lax.rsqrt(jnp.mean(xf * xf, axis=-1, keepdims=True) + EPS)
    return (y * g.astype(jnp.float32)).astype(x.dtype)


def layernorm(x, g, b):
    mu = jnp.mean(x, axis=-1, keepdims=True)
    var = jnp.mean(jnp.square(x - mu), axis=-1, keepdims=True)
    return (x - mu) * lax.rsqrt(var + EPS) * g + b


def head_groupnorm(x, g):
    mu = jnp.mean(x, axis=-1, keepdims=True)
    var = jnp.mean(jnp.square(x - mu), axis=-1, keepdims=True)
    return (x - mu) * lax.rsqrt(var + EPS) * g.reshape(x.shape[-2], x.shape[-1])


def l2norm(x):
    return x * lax.rsqrt(jnp.sum(x * x, axis=-1, keepdims=True) + EPS)


def causal_dwconv(x, w):
    k_size, ch = w.shape
    return lax.conv_general_dilated(x, w[:, None, :], window_strides=(1,),
                                    padding=[(k_size - 1, 0)],
                                    dimension_numbers=('NWC', 'WIO', 'NWC'),
                                    feature_group_count=ch)


def rope(x, positions):
    d = x.shape[-1]
    inv = ROPE_BASE ** (-jnp.arange(0, d, 2, dtype=jnp.float32) / d)
    ang = positions.astype(jnp.float32)[..., None] * inv
    cos = jnp.cos(ang)[:, :, None, :]
    sin = jnp.sin(ang)[:, :, None, :]
    x1, x2 = x[..., :d // 2], x[..., d // 2:]
    return jnp.concatenate([x1 * cos - x2 * sin, x1 * sin + x2 * cos], axis=-1)


def to_chunks(t, c):
    b, s, h, d = t.shape
    return t.reshape(b, s // c, c, h, d).transpose(0, 3, 1, 2, 4)


def from_chunks(t):
    b, h, n, c, d = t.shape
    return t.transpose(0, 2, 3, 1, 4).reshape(b, n * c, h, d)


def gated_delta_rule(q, k, v, g, beta):
    c = DN_CHUNK
    dk = q.shape[-1]
    q = to_chunks(q * dk ** -0.5, c)
    k = to_chunks(k, c)
    v = to_chunks(v, c)
    g = lax.cumsum(to_chunks(g[..., None], c)[..., 0], axis=3)
    beta = to_chunks(beta[..., None], c)[..., 0]
    k_beta = k * beta[..., None]
    causal = jnp.tril(jnp.ones((c, c), dtype=bool))
    decay = jnp.exp(jnp.where(causal, g[..., :, None] - g[..., None, :], -jnp.inf))
    kk = jnp.einsum('bhnid,bhnjd->bhnij', k_beta, k) * decay
    a_mat = jnp.eye(c, dtype=jnp.float32) + jnp.tril(kk, -1)
    u = lax.linalg.triangular_solve(a_mat, v * beta[..., None], left_side=True, lower=True)
    w = lax.linalg.triangular_solve(a_mat, k_beta * jnp.exp(g)[..., None], left_side=True, lower=True)
    qk = jnp.einsum('bhnid,bhnjd->bhnij', q, k) * decay
    g_last = g[..., -1:]
    k_state = k * jnp.exp(g_last - g)[..., None]
    q_state = q * jnp.exp(g)[..., None]

    def step(state, inp):
        qs, ks, u_i, w_i, qk_i, gl = inp
        v_new = u_i - jnp.einsum('bhcd,bhde->bhce', w_i, state)
        o = jnp.einsum('bhcd,bhde->bhce', qs, state) + jnp.einsum('bhij,bhje->bhie', qk_i, v_new)
        state = state * jnp.exp(gl)[..., None] + jnp.einsum('bhcd,bhce->bhde', ks, v_new)
        return state, o

    bsz, nh, _, _, dv = v.shape
    state0 = jnp.zeros((bsz, nh, dk, dv), jnp.float32)
    xs = tuple(jnp.moveaxis(t, 2, 0) for t in (q_state, k_state, u, w, qk, g_last))
    _, o = lax.scan(step, state0, xs)
    return from_chunks(jnp.moveaxis(o, 0, 2))


def retention(q, k, v):
    c = RET_CHUNK
    nh, dh = q.shape[2], q.shape[3]
    log_gamma = jnp.log1p(-jnp.exp2(-5.0 - jnp.arange(nh, dtype=jnp.float32)))
    q = to_chunks(q, c)
    k = to_chunks(k * dh ** -0.5, c)
    v = to_chunks(v, c)
    idx = jnp.arange(c, dtype=jnp.float32)
    rel = idx[:, None] - idx[None, :]
    decay = jnp.where(rel >= 0, jnp.exp(log_gamma[:, None, None] * jnp.maximum(rel, 0.0)), 0.0)
    scores = jnp.einsum('bhnid,bhnjd->bhnij', q, k) * decay[None, :, None]
    o_inner = jnp.einsum('bhnij,bhnje->bhnie', scores, v)
    zeta = jnp.exp(log_gamma[:, None] * (c - 1 - idx))
    kv = jnp.einsum('bhncd,bhnce->bhnde', k * zeta[None, :, None, :, None], v)
    gamma_c = jnp.exp(log_gamma * c)[None, :, None, None]

    def step(state, kv_i):
        return state * gamma_c + kv_i, state

    bsz = q.shape[0]
    _, prev = lax.scan(step, jnp.zeros((bsz, nh, dh, dh), jnp.float32), jnp.moveaxis(kv, 2, 0))
    prev = jnp.moveaxis(prev, 0, 2)
    xi = jnp.exp(log_gamma[:, None] * (idx + 1.0))
    o_cross = jnp.einsum('bhncd,bhnde->bhnce', q, prev) * xi[None, :, None, :, None]
    return from_chunks(o_inner + o_cross)


def multiscale_pool(p, pool_w, pool_scale):
    bsz, seq, _ = p.shape
    cs = lax.cumsum(p, axis=1)
    count_base = jnp.arange(1, seq + 1, dtype=jnp.float32)[:, None]
    outs = []
    for i, win in enumerate(POOL_WINDOWS):
        sl = slice(i * POOL_GROUP_DIM, (i + 1) * POOL_GROUP_DIM)
        cs_g = cs[..., sl]
        prev = jnp.pad(cs_g, ((0, 0), (win, 0), (0, 0)))[:, :seq]
        outs.append((cs_g - prev) / jnp.minimum(count_base, float(win)) - p[..., sl])
    pooled = jnp.stack(outs, axis=2)
    y = jnp.einsum('bsgc,gcd->bsgd', pooled, pool_w).reshape(bsz, seq, POOL_WIDTH)
    return y * pool_scale


def spatial_gating(su, sv, ln_g, ln_b, sg_w, sg_b):
    bsz, seq, _ = su.shape
    u = jax.nn.gelu(su, approximate=False)
    v = layernorm(jax.nn.gelu(sv, approximate=False), ln_g, ln_b)
    v = v.reshape(bsz, seq // SG_CHUNK, SG_CHUNK, SG_GROUPS, SG_GROUP_DIM)
    w_causal = jnp.tril(sg_w)
    mixed = jnp.einsum('gts,bnsgc->bntgc', w_causal, v) + sg_b.T[None, None, :, :, None]
    return u * mixed.reshape(bsz, seq, SG_WIDTH)


def hybrid_mixer(h, positions, w_in, dn_conv_w, dn_a_log, dn_dt_bias, dn_norm_g,
                 pool_w, pool_scale, sg_ln_g, sg_ln_b, sg_w, sg_b, ret_gn_g,
                 w_br_dn, w_br_pool, w_br_sg, w_br_ret, w_out):
    bsz, seq, _ = h.shape
    f32 = jnp.float32
    dt = h.dtype
    proj = h @ w_in
    cuts = np.cumsum(SPLIT_SIZES)[:-1].tolist()
    (dq, dk, dv, dz, db, da, pl, su, sv, rq, rk, rv, rg, gates) = jnp.split(proj, cuts, axis=-1)

    def heads(t, n, d):
        return t.astype(f32).reshape(bsz, seq, n, d)

    qkv = jax.nn.silu(causal_dwconv(jnp.concatenate([dq, dk, dv], axis=-1).astype(f32), dn_conv_w.astype(f32)))
    q_a, k_a, v_a = [heads(t, DN_HEADS, DN_HEAD_DIM) for t in jnp.split(qkv, 3, axis=-1)]
    beta = jax.nn.sigmoid(db.astype(f32))
    g_log = -jnp.exp(dn_a_log.astype(f32)) * jax.nn.softplus(da.astype(f32) + dn_dt_bias.astype(f32))
    o_a = gated_delta_rule(l2norm(q_a), l2norm(k_a), v_a, g_log, beta)
    o_a = rmsnorm(o_a, dn_norm_g) * jax.nn.silu(heads(dz, DN_HEADS, DN_HEAD_DIM))
    y_a = o_a.reshape(bsz, seq, DN_WIDTH)

    y_b = multiscale_pool(pl.astype(f32), pool_w.astype(f32), pool_scale.astype(f32))

    y_c = spatial_gating(su.astype(f32), sv.astype(f32), sg_ln_g.astype(f32), sg_ln_b.astype(f32),
                         sg_w.astype(f32), sg_b.astype(f32))

    q_r = rope(heads(rq, RET_HEADS, RET_HEAD_DIM), positions)
    k_r = rope(heads(rk, RET_HEADS, RET_HEAD_DIM), positions)
    o_r = head_groupnorm(retention(q_r, k_r, heads(rv, RET_HEADS, RET_HEAD_DIM)), ret_gn_g.astype(f32))
    y_d = jax.nn.silu(rg.astype(f32)) * o_r.reshape(bsz, seq, RET_WIDTH)

    gate = jax.nn.sigmoid(gates.astype(f32)).reshape(bsz, seq, N_BRANCH, D_MODEL).astype(dt)
    merged = (gate[:, :, 0] * (y_a.astype(dt) @ w_br_dn)
              + gate[:, :, 1] * (y_b.astype(dt) @ w_br_pool)
              + gate[:, :, 2] * (y_c.astype(dt) @ w_br_sg)
              + gate[:, :, 3] * (y_d.astype(dt) @ w_br_ret))
    return merged @ w_out


def setup_inputs(seed: int = 0) -> dict:
    key = jax.random.key(seed)
    keys = iter(jax.random.split(key, 32))
    f32 = jnp.float32
    L, D = DEPTH, D_MODEL

    def normal(shape, scale):
        return jax.random.normal(next(keys), shape, f32) * scale

    def gain(shape):
        return 1.0 + normal(shape, 0.02)

    x = normal((BATCH, SEQ, D), 1.0)
    c = normal((BATCH, D), 1.0)
    positions = (jnp.arange(SEQ, dtype=jnp.int32)[None, :]
                 + jax.random.randint(next(keys), (BATCH, 1), 0, 4096, dtype=jnp.int32))
    norm1_g = gain((L, D))
    norm2_g = gain((L, D))
    ada_w = normal((L, D, N_MOD * D), D ** -0.5)
    ada_b = normal((L, N_MOD * D), 0.02)
    w_in = normal((L, D, IN_COLS), D ** -0.5)
    dn_conv_w = normal((L, DN_CONV, 3 * DN_WIDTH), DN_CONV ** -0.5)
    dn_a_log = jnp.log(jax.random.uniform(next(keys), (L, DN_HEADS), f32, 1.0, 16.0))
    dt_init = jnp.exp(jax.random.uniform(next(keys), (L, DN_HEADS), f32, math.log(1e-3), math.log(1e-1)))
    dn_dt_bias = dt_init + jnp.log(-jnp.expm1(-dt_init))
    dn_norm_g = gain((L, DN_HEAD_DIM))
    pool_w = normal((L, POOL_GROUPS, POOL_GROUP_DIM, POOL_GROUP_DIM), POOL_GROUP_DIM ** -0.5)
    pool_scale = 1.0 + normal((L, POOL_WIDTH), 0.1)
    sg_ln_g = gain((L, SG_WIDTH))
    sg_ln_b = normal((L, SG_WIDTH), 0.02)
    sg_w = normal((L, SG_GROUPS, SG_CHUNK, SG_CHUNK), SG_CHUNK ** -0.5)
    sg_b = 1.0 + normal((L, SG_GROUPS, SG_CHUNK), 0.02)
    ret_gn_g = gain((L, RET_WIDTH))
    w_br_dn = normal((L, DN_WIDTH, D), DN_WIDTH ** -0.5)
    w_br_pool = normal((L, POOL_WIDTH, D), POOL_WIDTH ** -0.5)
    w_br_sg = normal((L, SG_WIDTH, D), SG_WIDTH ** -0.5)
    w_br_ret = normal((L, RET_WIDTH, D), RET_WIDTH ** -0.5)
    w_out = normal((L, D, D), D ** -0.5)
    mlp_w1 = normal((L, D, D_FF), D ** -0.5)
    mlp_w2 = normal((L, D_FF, D), D_FF ** -0.5)
    final_g = gain((D,))
    return {'x': x, 'c': c, 'positions': positions, 'norm1_g': norm1_g, 'norm2_g': norm2_g,
            'ada_w': ada_w, 'ada_b': ada_b, 'w_in': w_in, 'dn_conv_w': dn_conv_w,
            'dn_a_log': dn_a_log, 'dn_dt_bias': dn_dt_bias, 'dn_norm_g': dn_norm_g,
            'pool_w': pool_w, 'pool_scale': pool_scale, 'sg_ln_g': sg_ln_g, 'sg_ln_b': sg_ln_b,
            'sg_w': sg_w, 'sg_b': sg_b, 'ret_gn_g': ret_gn_g, 'w_br_dn': w_br_dn,
            'w_br_pool': w_br_pool, 'w_br_sg': w_br_sg, 'w_br_ret': w_br_ret, 'w_out': w_out,
            'mlp_w1': mlp_w1, 'mlp_w2': mlp_w2, 'final_g': final_g}


def reference(x, c, positions, norm1_g, norm2_g, ada_w, ada_b, w_in, dn_conv_w, dn_a_log,
              dn_dt_bias, dn_norm_g, pool_w, pool_scale, sg_ln_g, sg_ln_b, sg_w, sg_b, ret_gn_g,
              w_br_dn, w_br_pool, w_br_sg, w_br_ret, w_out, mlp_w1, mlp_w2, final_g):
    cond = jax.nn.silu(c)
    for l in range(DEPTH):
        mod = cond @ ada_w[l] + ada_b[l]
        shift1, scale1, gate1, shift2, scale2, gate2 = [m[:, None, :] for m in jnp.split(mod, N_MOD, axis=-1)]
        h = rmsnorm(x, norm1_g[l]) * (1.0 + scale1) + shift1
        x = x + gate1 * hybrid_mixer(h, positions, w_in[l], dn_conv_w[l], dn_a_log[l], dn_dt_bias[l],
                                     dn_norm_g[l], pool_w[l], pool_scale[l], sg_ln_g[l], sg_ln_b[l],
                                     sg_w[l], sg_b[l], ret_gn_g[l], w_br_dn[l], w_br_pool[l],
                                     w_br_sg[l], w_br_ret[l], w_out[l])
        h = rmsnorm(x, norm2_g[l]) * (1.0 + scale2) + shift2
        x = x + gate2 * (jnp.square(jax.nn.relu(h @ mlp_w1[l])) @ mlp_w2[l])
    return rmsnorm(x, final_g)
```

```python
import math
import numpy as np
import concourse.bass as bass
import concourse.mybir as mybir
from concourse.bass_utils import run_bass_kernel_spmd

F32 = mybir.dt.float32
BF16 = mybir.dt.bfloat16
I32 = mybir.dt.int32
AF = mybir.ActivationFunctionType
ALU = mybir.AluOpType
AX = mybir.AxisListType
ESZ = {F32: 4, BF16: 2, I32: 4}

ENGS = ("pe", "act", "dve", "pool", "sp")


class Op:
    __slots__ = ("eng", "emit", "deps", "signal", "count", "is_dma", "sem", "semval", "idx", "inc")

    def __init__(self, eng, emit, is_dma=False):
        self.eng = eng
        self.emit = emit
        self.deps = ()
        self.signal = False
        self.count = 0
        self.is_dma = is_dma
        self.sem = None
        self.semval = 0
        self.inc = 16


class Prog:
    def __init__(self, nc):
        self.nc = nc
        self.ops = []
        self.tinfo = {}
        self.hist = {}
        self.dma_keys = {}
        self.sem_cm = []
        self.single = False
        self.n_sems = 0

    def reg(self, handle, space, base, pbytes):
        self.tinfo[handle.name] = (space, base, pbytes)
        return handle

    def dram(self, name, shape, dtype, kind):
        t = self.nc.dram_tensor(name, list(shape), dtype, kind=kind)
        n = 1
        for s in shape:
            n *= s
        self.tinfo[t.name] = ("D:" + t.name, 0, n * ESZ[dtype])
        return t.ap()

    def region(self, ap):
        space, base, pbytes = self.tinfo[ap.tensor.name]
        esz = ESZ[ap.dtype]
        pairs = ap.ap
        off = ap.offset
        if space[0] == "D":
            ext = 1
            for s, c in pairs:
                ext += (c - 1) * abs(s)
            return (space, 0, 1, off * esz, (off + ext) * esz)
        ps = pbytes // esz
        p0 = off // ps
        f0 = off % ps
        s0, c0 = pairs[0]
        ext = 1
        for s, c in pairs[1:]:
            ext += (c - 1) * abs(s)
        if s0 != ps and c0 != 1:
            ext += (c0 - 1) * abs(s0)
            c0 = 1
        lo, hi = base + f0 * esz, base + (f0 + ext) * esz
        if space == "PS":
            lo = lo // 2048 * 2048
            hi = (hi + 2047) // 2048 * 2048
            return (space, 0, 128, lo, hi)
        return (space, p0, p0 + c0, lo, hi)

    def add(self, eng, emit, reads, writes, is_dma=False, key=None, inc=16):
        op = Op(eng, emit, is_dma)
        idx = len(self.ops)
        op.idx = idx
        self.ops.append(op)
        deps = set()
        rregs = [self.region(a) for a in reads]
        wregs = [self.region(a) for a in writes]
        for (sp, p0, p1, b0, b1) in rregs:
            for rec in self.hist.get(sp, ()):
                if rec[0] < p1 and p0 < rec[1] and rec[2] < b1 and b0 < rec[3]:
                    if rec[5] or (sp == "PS" and self.ops[rec[4]].eng != eng):
                        deps.add(rec[4])
        for (sp, p0, p1, b0, b1) in wregs:
            for rec in self.hist.get(sp, ()):
                if rec[0] < p1 and p0 < rec[1] and rec[2] < b1 and b0 < rec[3]:
                    deps.add(rec[4])
        for (sp, p0, p1, b0, b1) in wregs:
            lst = self.hist.setdefault(sp, [])
            lst[:] = [r for r in lst if not (p0 <= r[0] and r[1] <= p1 and b0 <= r[2] and r[3] <= b1)]
            lst.append([p0, p1, b0, b1, idx, True])
        for (sp, p0, p1, b0, b1) in rregs:
            lst = self.hist.setdefault(sp, [])
            if not is_dma:
                lst[:] = [r for r in lst if not ((not r[5]) and r[0] == p0 and r[1] == p1 and r[2] == b0
                                                 and r[3] == b1 and self.ops[r[4]].eng == eng
                                                 and not self.ops[r[4]].is_dma)]
            lst.append([p0, p1, b0, b1, idx, False])
        best = {}
        keep = []
        for d in deps:
            A = self.ops[d]
            if A.is_dma:
                keep.append(d)
            else:
                if A.eng == "pe" and eng == "pe" and not is_dma:
                    continue
                if A.eng not in best or best[A.eng] < d:
                    best[A.eng] = d
        keep.extend(best.values())
        for d in keep:
            self.ops[d].signal = True
        op.deps = tuple(sorted(keep))
        if is_dma:
            if key not in self.dma_keys:
                self.dma_keys[key] = [None, 0]
            ent = self.dma_keys[key]
            ent[1] += inc
            op.sem = key
            op.semval = ent[1]
            op.inc = inc
        return op

    def mm(self, out, lhsT, rhs, start=True, stop=True):
        return self.add("pe", lambda e: e.matmul(out, lhsT, rhs, start=start, stop=stop),
                        [lhsT, rhs], [out])

    def tr(self, out, in_, ident):
        return self.add("pe", lambda e: e.transpose(out, in_, ident), [in_, ident], [out])

    def act(self, out, in_, func, bias=None, scale=1.0, accum_out=None, eng="act"):
        reads = [in_]
        kw = {}
        if bias is not None:
            kw["bias"] = bias
            if not isinstance(bias, (int, float)):
                reads.append(bias)
        if not isinstance(scale, (int, float)):
            reads.append(scale)
        kw["scale"] = scale
        writes = [out]
        if accum_out is not None:
            kw["accum_out"] = accum_out
            writes.append(accum_out)
        return self.add("act", lambda e: e.activation(out, in_, func, **kw), reads, writes)

    def tt(self, eng, out, in0, in1, op):
        if eng == "pool":
            eng = "dve"
        return self.add(eng, lambda e: e.tensor_tensor(out, in0, in1, op), [in0, in1], [out])

    def ts(self, eng, out, in0, s1, s2=None, op0=ALU.mult, op1=None, accum_out=None):
        reads = [in0]
        if not isinstance(s1, (int, float)):
            reads.append(s1)
        if s2 is not None and not isinstance(s2, (int, float)):
            reads.append(s2)
        writes = [out]
        kw = {}
        if op1 is not None:
            kw["op1"] = op1
        if accum_out is not None:
            kw["accum_out"] = accum_out
            writes.append(accum_out)
        return self.add(eng, lambda e: e.tensor_scalar(out, in0, s1, s2, op0, **kw), reads, writes)

    def stt(self, eng, out, in0, scalar, in1, op0, op1):
        reads = [in0, in1]
        if not isinstance(scalar, (int, float)):
            reads.append(scalar)
        return self.add(eng, lambda e: e.scalar_tensor_tensor(out, in0, scalar, in1, op0, op1),
                        reads, [out])

    def copy(self, eng, out, in_):
        if eng == "pool":
            self.cp_i = getattr(self, "cp_i", 0) + 1
            eng = "act" if self.cp_i % 2 else "dve"
        if eng == "act":
            return self.add("act", lambda e: e.copy(out, in_), [in_], [out])
        return self.add(eng, lambda e: e.tensor_copy(out, in_), [in_], [out])

    def memset(self, eng, out, val):
        return self.add(eng, lambda e: e.memset(out, val), [], [out])

    def dma(self, q, out, in_, key):
        return self.add(q, lambda e: e.dma_start(out=out, in_=in_), [in_], [out], is_dma=True, key=key)

    def allgather(self, out, in_, groups, key):
        if self.single:
            n = in_.shape[0]
            for r in range(out.shape[0] // n):
                op = self.dma("sp", out[r * n:(r + 1) * n, :], in_, key=key + f"_{r}")
            return op
        return self.add("pool", lambda e: e.collective_compute("AllGather", ALU.bypass, replica_groups=groups,
                                                               ins=[in_], outs=[out]),
                        [in_], [out], is_dma=True, key=key, inc=1)

    def emit(self, final_keys=()):
        nc = self.nc
        from contextlib import ExitStack
        with ExitStack() as st:
            engsem = {}
            for e in ENGS[:4]:
                engsem[e] = st.enter_context(nc.semaphore("s_" + e))
            for k, ent in self.dma_keys.items():
                ent[0] = st.enter_context(nc.semaphore("d_" + k))
            counter = {e: 0 for e in ENGS}
            for op in self.ops:
                if not op.is_dma and op.signal:
                    counter[op.eng] += 1
                    op.count = counter[op.eng]
            known = {e: {} for e in ENGS}
            plan = {e: [] for e in ENGS}
            for op in self.ops:
                waits = []
                kn = known[op.eng]
                for d in op.deps:
                    A = self.ops[d]
                    if A.is_dma:
                        sk = "d_" + A.sem
                        sem = self.dma_keys[A.sem][0]
                        val = A.semval
                    else:
                        sk = A.eng
                        sem = engsem[A.eng]
                        val = A.count
                    if kn.get(sk, 0) >= val:
                        continue
                    kn[sk] = val
                    waits.append((sem, val))
                plan[op.eng].append((op, waits))
            fin = [(self.dma_keys[k][0], self.dma_keys[k][1]) for k in final_keys]
            block = st.enter_context(nc.Block())

            def run(engname):
                def body(e):
                    for op, waits in plan[engname]:
                        for sem, val in waits:
                            e.wait_ge(sem, val)
                        ins = op.emit(e)
                        if op.is_dma:
                            ins.then_inc(self.dma_keys[op.sem][0], op.inc)
                        elif op.signal:
                            ins.then_inc(engsem[engname], 1)
                    if engname == "sp":
                        for sem, val in fin:
                            e.wait_ge(sem, val)
                return body

            block.tensor(run("pe"))
            block.scalar(run("act"))
            block.vector(run("dve"))
            block.gpsimd(run("pool"))
            block.sync(run("sp"))


D = 1024
NB = 2
SEQ = 8192
DEPTH = 2
NCORE = 8
RPB = 4
T = SEQ // RPB
HALO = 16
NH = 4
IN_COLS = 9736
C_DQ, C_DK, C_DV, C_DZ, C_DB, C_DA = 0, 512, 1024, 1536, 2048, 2052
C_PL, C_SU, C_SV = 2056, 2568, 3080
C_RQ, C_RK, C_RV, C_RG = 3592, 4104, 4616, 5128
C_GATES = 5640
EPS = 1e-6
GROUPS = [[0, 1, 2, 3], [4, 5, 6, 7]]
DEBUG = {}


class Ctx:
    pass


def _prod(s):
    n = 1
    for v in s:
        n *= v
    return n


def build(stop_after=None, dumps=(), single=False):
    nc = bass.Bass("TRN2", target_bir_lowering=False)
    P = Prog(nc)
    P.single = single
    K = Ctx()
    K.nc, K.P = nc, P
    K.dumps = {}
    K.dump_req = set(dumps)

    def din(name, shape, dt=F32):
        return P.dram(name, shape, dt, "ExternalInput")

    I = {}
    I["x"] = din("x", [T, D])
    I["xh"] = din("xh", [HALO, D])
    I["cT"] = din("cT", [128, 8])
    I["pos"] = din("pos", [1, T], I32)
    I["meta"] = din("meta", [128, 8])
    I["norm1_g"] = din("norm1_g", [DEPTH, D])
    I["norm2_g"] = din("norm2_g", [DEPTH, D])
    I["ada_w"] = din("ada_w", [DEPTH, D, 6 * D])
    I["ada_b"] = din("ada_b", [DEPTH, 6 * D])
    I["w_in"] = din("w_in", [DEPTH, D, IN_COLS])
    I["dn_conv_w"] = din("dn_conv_w", [DEPTH, 4, 1536])
    I["dn_a_log"] = din("dn_a_log", [DEPTH, 4])
    I["dn_dt_bias"] = din("dn_dt_bias", [DEPTH, 4])
    I["dn_norm_g"] = din("dn_norm_g", [DEPTH, 128])
    I["pool_w"] = din("pool_w", [DEPTH, 4, 128, 128])
    I["pool_scale"] = din("pool_scale", [DEPTH, 512])
    I["sg_ln_g"] = din("sg_ln_g", [DEPTH, 512])
    I["sg_ln_b"] = din("sg_ln_b", [DEPTH, 512])
    I["sg_w"] = din("sg_w", [DEPTH, 4, 128, 128])
    I["sg_b"] = din("sg_b", [DEPTH, 4, 128])
    I["ret_gn_g"] = din("ret_gn_g", [DEPTH, 512])
    I["w_br_dn"] = din("w_br_dn", [DEPTH, 512, D])
    I["w_br_pool"] = din("w_br_pool", [DEPTH, 512, D])
    I["w_br_sg"] = din("w_br_sg", [DEPTH, 512, D])
    I["w_br_ret"] = din("w_br_ret", [DEPTH, 512, D])
    I["w_out"] = din("w_out", [DEPTH, D, D])
    I["mlp_w1"] = din("mlp_w1", [DEPTH, D, 4 * D])
    I["mlp_w2"] = din("mlp_w2", [DEPTH, 4 * D, D])
    I["final_g"] = din("final_g", [1, D])
    K.I = I
    K.out = P.dram("out", [T, D], F32, "ExternalOutput")
    K.dn_o = P.dram("dn_o", [NH, 128, T], F32, "Internal")
    K.dn_b = P.dram("dn_b", [NH, 128, T], F32, "Internal")
    K.ret_o = P.dram("ret_o", [NH, 128, T], F32, "Internal")
    K.ret_q = P.dram("ret_q", [NH, 128, T], F32, "Internal")
    K.cs_dram = P.dram("cs_dram", [2, 128, T], F32, "Internal")
    K.st_src = P.dram("st_src", [12 * 128, 128], F32, "Internal")
    K.st_all = P.dram("st_all", [RPB * 12 * 128, 128], F32, "Internal")
    K.xh_src = P.dram("xh_src", [D, HALO], F32, "Internal")
    K.xh_all = P.dram("xh_all", [RPB * D, HALO], F32, "Internal")

    slab = nc.alloc_sbuf_tensor("slab", [128, 8], F32)
    base0 = nc.lookup_mloc(slab).addr
    K.sb_cur = base0 + 32
    K.sb_lim = base0 + 212000
    K.sb_n = 0

    def sb(name, shape, dt=F32, at=None):
        pbytes = _prod(shape[1:]) * ESZ[dt]
        off = K.sb_cur if at is None else at
        K.sb_n += 1
        h = nc.alloc_sbuf_tensor_at(f"{name}_{K.sb_n}", list(shape), dt, offset=off)
        P.reg(h, "SB", off, pbytes)
        if at is None:
            K.sb_cur += (pbytes + 31) // 32 * 32
            assert K.sb_cur <= K.sb_lim, f"SBUF overflow at {name}: {K.sb_cur - base0}"
        return h

    K.sb = sb
    K.psb = []
    for i in range(8):
        h = nc.alloc_psum_tensor(f"psb{i}", [128, 512], F32)
        P.reg(h, "PS", i * 2048, 2048)
        K.psb.append(h)
    K.ps_big_i = 0
    K.ps_small_i = 0

    def ps_big():
        K.ps_big_i = (K.ps_big_i + 1) % 4
        return K.psb[K.ps_big_i]

    def ps_small():
        K.ps_small_i = (K.ps_small_i + 1) % 16
        b, q = divmod(K.ps_small_i, 4)
        return K.psb[4 + b][:, q * 128:(q + 1) * 128]

    K.ps_big, K.ps_small = ps_big, ps_small
    K.ps_i = 0

    def ps_tile():
        K.ps_i = (K.ps_i + 1) % 8
        return K.psb[K.ps_i]

    K.ps_tile = ps_tile

    def dump(name, ap, shape, dt=F32):
        if name not in K.dump_req:
            return
        d = P.dram("dbg_" + name, list(shape), dt, "ExternalOutput")
        K.dumps[name] = d
        P.dma("sp", d, ap, key="dbg_" + name)

    K.dump = dump

    stages(K, stop_after)

    fk = [k for k in P.dma_keys if k.startswith("out")]
    fk += ["dbg_" + n for n in K.dumps]
    P.emit(final_keys=fk)
    return nc, K


def stages(K, stop_after):
    P, sb, I = K.P, K.sb, K.I
    C = Ctx()
    K.C = C
    di = sb("c_di", [128, 128], I32)
    df = sb("c_df", [128, 128], F32)
    P.add("pool", lambda e: e.iota(di[:], [[1, 128]], base=0, channel_multiplier=-1), [], [di[:]])
    P.copy("dve", df[:], di[:])
    C.ident = sb("c_ident", [128, 128], F32)
    C.U = sb("c_U", [128, 128], F32)
    C.negmT = sb("c_negmT", [128, 128], F32)
    C.posm = sb("c_posm", [128, 128], F32)
    C.ones = sb("c_ones", [128, 128], F32)
    C.identb = sb("c_identb", [128, 128], BF16)
    P.ts("dve", C.ident[:], df[:], 0.0, None, op0=ALU.is_equal)
    P.ts("dve", C.U[:], df[:], 0.0, None, op0=ALU.is_ge)
    P.ts("dve", C.negmT[:], df[:], 0.0, -30000.0, op0=ALU.is_lt, op1=ALU.mult)
    P.ts("dve", C.posm[:], df[:], 0.0, -30000.0, op0=ALU.is_ge, op1=ALU.mult)
    C.SL = sb("c_SL", [128, 128], F32)
    P.ts("dve", C.SL[:], df[:], 0.0, None, op0=ALU.is_lt)
    P.memset("pool", C.ones[:], 1.0)
    P.copy("dve", C.identb[:], C.ident[:])
    pi_ = sb("c_pi", [128, 1], I32)
    pm_i = sb("c_pmi", [128, 1], I32)
    pm = sb("c_pm", [128, 1], F32)
    lo = sb("c_lo", [128, 1], F32)
    hi = sb("c_hi", [128, 1], F32)
    t1 = sb("c_t1", [128, 128], F32)
    bd = {}
    P.add("pool", lambda e: e.iota(pi_[:], [[0, 1]], base=0, channel_multiplier=1), [], [pi_[:]])
    for sz in (16, 32, 64):
        bd[sz] = sb(f"c_bd{sz}", [128, 128], F32)
        P.ts("dve", pm_i[:], pi_[:], sz - 1, None, op0=ALU.bitwise_and)
        P.copy("dve", pm[:], pm_i[:])
        P.ts("dve", lo[:], pm[:], -1.0, None, op0=ALU.mult)
        P.ts("dve", hi[:], pm[:], -1.0, float(sz), op0=ALU.mult, op1=ALU.add)
        P.ts("dve", bd[sz][:], df[:], lo[:, 0:1], None, op0=ALU.is_ge)
        P.ts("dve", t1[:], df[:], hi[:, 0:1], None, op0=ALU.is_lt)
        P.tt("dve", bd[sz][:], bd[sz][:], t1[:], ALU.mult)
    C.bd16 = bd[16]
    C.em = [sb(f"c_em{i}", [128, 128], F32) for i in range(3)]
    P.tt("dve", C.em[0][:], bd[32][:], bd[16][:], ALU.subtract)
    P.tt("dve", C.em[1][:], bd[64][:], bd[32][:], ALU.subtract)
    P.ts("dve", C.em[2][:], bd[64][:], -1.0, 1.0, op0=ALU.mult, op1=ALU.add)
    K.dump("ident", C.ident[:], [128, 128])
    K.dump("em0", C.em[0][:], [128, 128])
    K.dump("negmT", C.negmT[:], [128, 128])

    K.xT = sb("xT", [128, 8, HALO + T], F32)
    K.mark0 = K.sb_cur
    load_x(K)
    K.dump("xT", K.xT[:], [128, 8, HALO + T])
    if stop_after == "load_x":
        return
    preamble(K)
    if stop_after == "preamble":
        return
    for l in range(DEPTH):
        layer(K, l, stop_after)
        if stop_after is not None and stop_after.startswith(f"L{l}"):
            return
    final_norm(K)


def load_x(K):
    P, sb, I, C = K.P, K.sb, K.I, K.C
    xin = [sb(f"xin{i}", [128, D], F32) for i in range(2)]
    for tt in range(T // 128):
        xt = xin[tt % 2]
        P.dma("sp", xt[:], I["x"][tt * 128:(tt + 1) * 128, :], key=f"xin{tt % 2}")
        for fc in range(8):
            ps = K.ps_small()
            P.tr(ps, xt[:, fc * 128:(fc + 1) * 128], C.ident[:])
            P.copy("act" if fc % 2 else "dve", K.xT[:, fc, HALO + tt * 128:HALO + (tt + 1) * 128], ps)
    xh = sb("xh", [HALO, D], F32)
    P.dma("sp", xh[:], I["xh"][:, :], key="xh")
    for fc in range(8):
        ps = K.ps_small()
        P.tr(ps[:, 0:HALO], xh[:, fc * 128:(fc + 1) * 128], C.ident[0:HALO, 0:HALO])
        P.copy("dve", K.xT[:, fc, 0:HALO], ps[:, 0:HALO])
    K.sb_cur = K.mark0


def preamble(K):
    P, sb, I, C = K.P, K.sb, K.I, K.C
    NV = 121
    K.VT = [sb(f"VT{l}", [128, NV], F32) for l in range(DEPTH)]
    K.fgT = sb("fgT", [128, 8], F32)
    K.modT = [sb(f"modT{l}", [128, 48], F32) for l in range(DEPTH)]
    K.AB = [sb(f"AB{l}", [128, 32], F32) for l in range(DEPTH)]
    K.cond = sb("cond", [128, 8], F32)
    mark = K.sb_cur
    rows = sb("vrows", [128, 128], F32)
    for l in range(DEPTH):
        srcs = [(0, 8, I["norm1_g"][l].rearrange("(r p) -> r p", p=128)),
                (8, 8, I["norm2_g"][l].rearrange("(r p) -> r p", p=128)),
                (16, 48, I["ada_b"][l].rearrange("(r p) -> r p", p=128)),
                (64, 48, I["dn_conv_w"][l].rearrange("k (c p) -> (k c) p", p=128)),
                (112, 1, I["dn_norm_g"][l].rearrange("(r p) -> r p", p=128)),
                (113, 4, I["pool_scale"][l].rearrange("(r p) -> r p", p=128)),
                (117, 4, I["ret_gn_g"][l].rearrange("(r p) -> r p", p=128))]
        for (r0, n, src) in srcs:
            P.dma("sp", rows[r0:r0 + n, :], src, key=f"vrows{r0}")
        ps = K.ps_small()
        P.tr(ps[:, 0:NV], rows[0:NV, :], C.ident[0:NV, 0:NV])
        P.copy("dve", K.VT[l][:, :], ps[:, 0:NV])
    P.dma("sp", rows[0:8, :], I["final_g"][0].rearrange("(r p) -> r p", p=128), key="vrows0")
    ps = K.ps_small()
    P.tr(ps[:, 0:8], rows[0:8, :], C.ident[0:8, 0:8])
    P.copy("dve", K.fgT[:, :], ps[:, 0:8])
    ct = sb("ct", [128, 8], F32)
    P.dma("sp", ct[:], I["cT"][:, :], key="ct")
    P.act(K.cond[:], ct[:], AF.Silu)
    PC = 768
    wa = [sb(f"wada{i}", [128, 8, PC], F32) for i in range(2)]
    n = 0
    rowbuf = sb("modrow", [1, 6 * D], F32)
    for l in range(DEPTH):
        wsrc = I["ada_w"][l].rearrange("(kc p) n -> p kc n", p=128)
        for pc in range(6 * D // PC):
            wt = wa[n % 2]
            P.dma("sp", wt[:], wsrc[:, :, pc * PC:(pc + 1) * PC], key=f"wada{n % 2}")
            n += 1
            for hf in range(2):
                psr = K.ps_tile()
                c0_ = hf * (PC // 2)
                for kc in range(8):
                    P.mm(psr[0:1, 0:PC // 2], K.cond[:, kc:kc + 1], wt[:, kc, c0_:c0_ + PC // 2],
                         start=(kc == 0), stop=(kc == 7))
                P.copy("act" if hf else "dve", rowbuf[0:1, pc * PC + c0_:pc * PC + c0_ + PC // 2], psr[0:1, 0:PC // 2])
        psm = K.ps_tile()
        for oc in range(48):
            P.tr(psm[:, oc:oc + 1], rowbuf[0:1, oc * 128:(oc + 1) * 128], C.ident[0:1, 0:1])
        P.tt("dve", K.modT[l][:, :], psm[:, 0:48], K.VT[l][:, 16:64], ALU.add)
        P.stt("dve", K.AB[l][:, 0:8], K.modT[l][:, 8:16], 1.0, K.VT[l][:, 0:8], ALU.add, ALU.mult)
        P.stt("dve", K.AB[l][:, 16:24], K.modT[l][:, 32:40], 1.0, K.VT[l][:, 8:16], ALU.add, ALU.mult)
        K.dump(f"modT{l}", K.modT[l][:, :], [128, 48])
    K.sb_cur = mark
    K.meta = sb("meta", [128, 8], F32)
    P.dma("sp", K.meta[:], I["meta"][:, :], key="meta")
    K.maskj = sb("maskj", [128, 1], F32)
    P.ts("dve", K.maskj[:], K.meta[:, 0:1], 1.0, None, op0=ALU.min)
    K.invc = sb("invc", [128, NH, HALO], F32)
    fi = sb("iv_fi", [128, HALO], I32)
    ff = sb("iv_ff", [128, HALO], F32)
    tb = sb("iv_tb", [128, HALO], F32)
    P.add("pool", lambda e: e.iota(fi[:], [[1, HALO]], base=1, channel_multiplier=0), [], [fi[:]])
    P.copy("dve", ff[:], fi[:])
    for g in range(NH):
        win = float(2 << g)
        P.ts("dve", tb[:], ff[:], win, None, op0=ALU.min)
        P.add("dve", lambda e: e.reciprocal(tb[:], tb[:]), [tb[:]], [tb[:]])
        P.ts("dve", K.invc[:, g, :], tb[:], -1.0, 1.0 / win, op0=ALU.mult, op1=ALU.add)
        P.stt("dve", K.invc[:, g, :], K.invc[:, g, :], K.maskj[:, 0:1], tb[:], ALU.mult, ALU.add)
    K.rstd = sb("rstd", [128, HALO + T], F32)
    rope_ret_consts(K)


LOG_GAMMA = [math.log1p(-2.0 ** (-5.0 - h)) for h in range(NH)]


def rope_ret_consts(K):
    P, sb, I, C = K.P, K.sb, K.I, K.C
    C.Rm = sb("c_Rm", [128, 128], F32)
    C.decT = sb("c_decT", [128, NH, 128], F32)
    C.xi = sb("c_xi", [128, NH, 128], F32)
    C.zeta = sb("c_zeta", [128, NH], F32)
    C.negpi = sb("c_negpi", [128, 1], F32)
    P.memset("pool", C.negpi[:], -math.pi)
    mark = K.sb_cur
    C.cosT = sb("cosT", [128, T], F32)
    C.sinT = sb("sinT", [128, T], F32)
    df = sb("t_df", [128, 128], F32)
    di = sb("t_di", [128, 128], I32)
    t1 = sb("t_t1", [128, 128], F32)
    P.add("pool", lambda e: e.iota(di[:], [[1, 128]], base=0, channel_multiplier=-1), [], [di[:]])
    P.copy("dve", df[:], di[:])
    P.ts("dve", C.Rm[:], df[:], 64.0, None, op0=ALU.is_equal)
    P.ts("dve", t1[:], df[:], -64.0, None, op0=ALU.is_equal)
    P.tt("dve", C.Rm[:], C.Rm[:], t1[:], ALU.subtract)
    fi = sb("t_fi", [128, 128], I32)
    ff = sb("t_ff", [128, 128], F32)
    P.add("pool", lambda e: e.iota(fi[:], [[1, 128]], base=1, channel_multiplier=0), [], [fi[:]])
    P.copy("dve", ff[:], fi[:])
    pi_ = sb("t_pi", [128, 1], I32)
    pf = sb("t_pf", [128, 1], F32)
    P.add("pool", lambda e: e.iota(pi_[:], [[0, 1]], base=127, channel_multiplier=-1), [], [pi_[:]])
    P.copy("dve", pf[:], pi_[:])
    for h in range(NH):
        lg = LOG_GAMMA[h]
        P.act(t1[:], df[:], AF.Exp, scale=lg)
        P.stt("dve", C.decT[:, h, :], t1[:], 128.0 ** -0.5, C.U[:], ALU.mult, ALU.mult)
        P.act(C.xi[:, h, :], ff[:], AF.Exp, scale=lg)
        P.act(C.zeta[:, h:h + 1], pf[:], AF.Exp, scale=lg)
    P.ts("dve", C.zeta[:, :], C.zeta[:, :], 128.0 ** -0.5, None, op0=ALU.mult)
    inv = sb("t_inv", [128, 1], F32)
    meta2 = sb("t_meta", [128, 8], F32)
    P.dma("sp", meta2[:], I["meta"][:, :], key="meta2")
    P.copy("dve", inv[:], meta2[:, 2:3])
    posi = sb("t_posi", [128, T], I32)
    P.dma("sp", posi[:], I["pos"][0:1, :].to_broadcast([128, T]), key="posi")
    ang = sb("t_ang", [128, T], F32)
    P.copy("dve", ang[:], posi[:])
    P.ts("dve", ang[:], ang[:], inv[:, 0:1], None, op0=ALU.mult)
    TWO_PI = 2.0 * math.pi
    ki = sb("t_ki", [128, T], I32)
    kf = sb("t_kf", [128, T], F32)

    def sin_table(dst, shift):
        a = dst
        if shift != 0.0:
            P.ts("dve", a, ang[:], shift, None, op0=ALU.add)
        else:
            P.copy("dve", a, ang[:])
        P.ts("dve", kf[:], a, 1.0 / TWO_PI, None, op0=ALU.mult)
        P.copy("dve", ki[:], kf[:])
        P.copy("dve", kf[:], ki[:])
        P.stt("dve", a, kf[:], -6.28125, a, ALU.mult, ALU.add)
        P.stt("dve", a, kf[:], -(TWO_PI - 6.28125), a, ALU.mult, ALU.add)
        P.ts("dve", kf[:], a, math.pi, -TWO_PI, op0=ALU.is_gt, op1=ALU.mult)
        P.tt("dve", a, a, kf[:], ALU.add)
        P.ts("dve", kf[:], a, -math.pi, TWO_PI, op0=ALU.is_lt, op1=ALU.mult)
        P.tt("dve", a, a, kf[:], ALU.add)
        P.ts("dve", a, a, -math.pi, math.pi, op0=ALU.max, op1=ALU.min)
        P.act(a, a, AF.Sin)

    sin_table(C.sinT[:], 0.0)
    sin_table(C.cosT[:], math.pi / 2.0)
    P.dma("sp", K.cs_dram[0], C.cosT[:], key="cs_w0")
    P.dma("sp", K.cs_dram[1], C.sinT[:], key="cs_w1")
    K.dump("cosT", C.cosT[:], [128, T])
    K.dump("sinT", C.sinT[:], [128, T])
    K.dump("decT", C.decT[:], [128, NH, 128])
    K.sb_cur = mark


def rsqrt(K, out, in_, mul, add):
    P = K.P
    P.act(out, in_, AF.Ln, bias=float(add), scale=float(mul))
    P.act(out, out, AF.Exp, scale=-0.5)


def norm_h(K, A, B, col0, ncol, hT, hcol0, compute_rstd=True):
    P, C = K.P, K.C
    c = 0
    while c < ncol:
        n = min(512, ncol - c)
        xs = lambda fc: K.xT[:, fc, col0 + c:col0 + c + n]
        rs = K.rstd[:, col0 + c:col0 + c + n]
        if compute_rstd:
            ps = K.ps_big()
            for fc in range(8):
                sq = K.sq[fc % 2]
                P.act(sq[:, 0:n], xs(fc), AF.Square)
                P.mm(ps[:, 0:n], C.ones[:, :], sq[:, 0:n], start=(fc == 0), stop=(fc == 7))
            rsqrt(K, rs, ps[:, 0:n], 1.0 / D, EPS)
        for fc in range(8):
            tmp = K.sq[fc % 2]
            P.tt("dve" if fc % 2 else "pool", tmp[:, 0:n], xs(fc), rs, ALU.mult)
            P.act(hT[:, fc, hcol0 + c:hcol0 + c + n], tmp[:, 0:n], AF.Identity, bias=B[:, fc:fc + 1],
                  scale=A[:, fc:fc + 1])
        c += n


def load_w(K, wt, src2d, c0, ncols, key):
    nk = src2d.shape[0] // 128
    src = src2d.rearrange("(kc p) n -> p kc n", p=128)[:, :, c0:c0 + ncols]
    K.P.dma("pool", wt[:, 0:nk, 0:ncols], src, key=key)


class WStream:
    def __init__(self, bufs, loadfns, dist):
        self.bufs, self.fns, self.dist = bufs, loadfns, dist
        self.issued = 0
        for _ in range(min(dist, len(loadfns))):
            self._issue()

    def _issue(self):
        i = self.issued
        self.fns[i](self.bufs[i % len(self.bufs)], i % len(self.bufs))
        self.issued += 1

    def get(self, i):
        while self.issued <= min(i + self.dist, len(self.fns) - 1):
            self._issue()
        return self.bufs[i % len(self.bufs)]


def layer_setup(K, l):
    P, sb, I, C = K.P, K.sb, K.I, K.C
    L = K.L
    L.dtb = sb("dtb", [128, 4], F32)
    L.nexpA = sb("nexpA", [128, 4], F32)
    P.dma("sp", L.dtb[:], I["dn_dt_bias"][l:l + 1, :].to_broadcast([128, 4]), key="dtb")
    P.dma("sp", L.nexpA[:], I["dn_a_log"][l:l + 1, :].to_broadcast([128, 4]), key="alog")
    P.act(L.nexpA[:], L.nexpA[:], AF.Exp)
    P.ts("dve", L.nexpA[:], L.nexpA[:], -1.0, None, op0=ALU.mult)
    NT = T // 128
    L.beta = sb("beta", [128, NT, 4], F32)
    L.nbeta = sb("nbeta", [128, NT, 4], F32)
    L.g = sb("g", [128, NT, 4], F32)
    L.gc = sb("gc", [128, NT, 4], F32)
    L.ngc = sb("ngc", [128, NT, 4], F32)
    L.bg = sb("bg", [128, NT, 4], F32)
    L.S = [sb(f"S{h}", [128, 128], F32) for h in range(NH)]
    L.Pref = [[sb(f"Pref{h}_{i}", [128, 128], F32) for i in range(2)] for h in range(NH)]
    L.pref_i = [0] * NH
    L.carry = sb("carry", [128, NH, 3, HALO], F32)
    for h in range(NH):
        P.memset("pool", L.S[h][:], 0.0)
        P.copy("pool", L.Pref[h][0][:], C.ident[:])
    L.Sr = [sb(f"Sr{h}", [128, 128], F32) for h in range(NH)]
    for h in range(NH):
        P.memset("pool", L.Sr[h][:], 0.0)
    L.wbg = sb("wbg", [128, 8, 8], BF16)
    load_w(K, L.wbg, I["w_in"][l], C_DB, 8, key="wbg")


def layer_setup_c(K, l):
    P, sb, I, C = K.P, K.sb, K.I, K.C
    L = K.L
    L.Sin = sb("Sin", [128, NH, 128], F32)
    L.Srin = sb("Srin", [128, NH, 128], F32)
    L.pcarry = sb("pcarry", [128, NH, HALO], F32)
    L.lng = sb("lng", [128, 512], F32)
    L.lnb = sb("lnb", [128, 512], F32)
    P.dma("sp", L.lng[:], I["sg_ln_g"][l:l + 1, :].to_broadcast([128, 512]), key="lng")
    P.dma("sp", L.lnb[:], I["sg_ln_b"][l:l + 1, :].to_broadcast([128, 512]), key="lnb")
    L.sgb = sb("sgb", [128, NH * 128], F32)
    P.dma("sp", L.sgb[:], I["sg_b"][l:l + 1].rearrange("o g n -> o (g n)").to_broadcast([128, NH * 128]), key="sgb")
    L.WcT = sb("WcT", [128, NH, 128], BF16)
    m_ = K.sb_cur
    sgw = sb("sgw", [128, NH, 128], F32)
    P.dma("sp", sgw[:], I["sg_w"][l].rearrange("g t s -> t g s"), key="sgw")
    psw = K.ps_tile()
    for g in range(NH):
        P.tr(psw[:, g * 128:(g + 1) * 128], sgw[:, g, :], C.ident[:])
    for g in range(NH):
        P.tt("dve", L.WcT[:, g, :], psw[:, g * 128:(g + 1) * 128], C.U[:], ALU.mult)
    K.sb_cur = m_


def bg_unit(K, hT, hh):
    P, C, L = K.P, K.C, K.L
    ps = K.ps_tile()
    for tt in range(8):
        for kc in range(8):
            P.mm(ps[:, tt * 8:tt * 8 + 8], hT[:, kc, HALO + tt * 128:HALO + (tt + 1) * 128], L.wbg[:, kc, 0:8],
                 start=(kc == 0), stop=(kc == 7))
    pv = ps[:, 0:64].rearrange("p (t c) -> p t c", c=8)
    sl = slice(hh * 8, hh * 8 + 8)
    P.act(L.beta[:, sl, :], pv[:, :, 0:4], AF.Sigmoid)
    P.ts("dve", L.nbeta[:, sl, :], L.beta[:, sl, :], -1.0, None, op0=ALU.mult)
    tmp = L.bg[:, sl, :]
    for tt in range(8):
        P.tt("dve", L.g[:, hh * 8 + tt, :], pv[:, tt, 4:8], L.dtb[:, :], ALU.add)
    P.act(L.g[:, sl, :], L.g[:, sl, :], AF.Exp)
    P.act(L.g[:, sl, :], L.g[:, sl, :], AF.Ln, bias=1.0)
    for tt in range(8):
        P.tt("dve", L.g[:, hh * 8 + tt, :], L.g[:, hh * 8 + tt, :], L.nexpA[:, :], ALU.mult)
    ps2 = K.ps_tile()
    for tt in range(8):
        P.mm(ps2[:, tt * 4:tt * 4 + 4], C.U[:, :], L.g[:, hh * 8 + tt, :])
    pv2 = ps2[:, 0:32].rearrange("p (t c) -> p t c", c=4)
    P.copy("dve", L.gc[:, sl, :], pv2)
    P.ts("dve", L.ngc[:, sl, :], pv2, -1.0, None, op0=ALU.mult)
    P.act(tmp, pv2, AF.Exp)
    P.tt("dve", L.bg[:, sl, :], tmp, L.beta[:, sl, :], ALU.mult)


def dn_head_unit(K, hT, hh, h):
    P, sb, I, C, L = K.P, K.sb, K.I, K.C, K.L
    l = L.l
    HT = 1024
    mark = K.sb_cur
    alias_base = K.sb_cur
    w3 = sb("w3", [128, 3, 8, 128], BF16)
    for ch, c0 in enumerate((C_DQ, C_DK, C_DV)):
        load_w(K, w3[:, ch], I["w_in"][l], c0 + h * 128, 128, key=f"w3_{ch}")
    pre = sb("pre", [128, 3, HALO + HT], F32)
    alias_end = K.sb_cur
    post = sb("post", [128, 3, HT], F32)
    ostage = sb("ostage", [128, HT], F32)
    bstage = sb("bstage", [128, HT], F32)
    for ch in range(3):
        if hh == 0:
            ps = K.ps_tile()
            for kc in range(8):
                P.mm(ps[:, 0:HALO], w3[:, ch, kc, :], hT[:, kc, 0:HALO], start=(kc == 0), stop=(kc == 7))
            P.ts("dve", pre[:, ch, 0:HALO], ps[:, 0:HALO], K.maskj[:, 0:1], None, op0=ALU.mult)
        else:
            P.copy("pool", pre[:, ch, 0:HALO], L.carry[:, h, ch, :])
        for gi in range(2):
            ps = K.ps_tile()
            for kc in range(8):
                P.mm(ps[:, :], w3[:, ch, kc, :], hT[:, kc, HALO + gi * 512:HALO + (gi + 1) * 512],
                     start=(kc == 0), stop=(kc == 7))
            P.copy("act", pre[:, ch, HALO + gi * 512:HALO + (gi + 1) * 512], ps[:, :])
        if hh == 0:
            P.copy("pool", L.carry[:, h, ch, :], pre[:, ch, HT:HT + HALO])
    for ch in range(3):
        cc = ch * 4 + h
        wcol = lambda k: K.VT[l][:, 64 + k * 12 + cc:64 + k * 12 + cc + 1]
        dst = post[:, ch, :]
        P.act(dst, pre[:, ch, HALO - 3:HALO - 3 + HT], AF.Copy, scale=wcol(0))
        for k in range(1, 4):
            P.stt("dve", dst, pre[:, ch, HALO - 3 + k:HALO - 3 + k + HT], wcol(k), dst, ALU.mult, ALU.add)
        P.act(dst, dst, AF.Silu)
    for ch in range(2):
        for gi in range(2):
            seg = post[:, ch, gi * 512:(gi + 1) * 512]
            sq = K.sq[gi % 2]
            P.act(sq[:, :], seg, AF.Square)
            ps = K.ps_tile()
            P.mm(ps[:, :], C.ones[:, :], sq[:, :])
            rsqrt(K, sq[:, :], ps[:, :], 1.0, EPS)
            if ch == 0:
                P.stt("dve", seg, seg, 128.0 ** -0.5, sq[:, :], ALU.mult, ALU.mult)
            else:
                P.tt("pool", seg, seg, sq[:, :], ALU.mult)
    K.dump(f"dnq{l}_{hh}_{h}", post[:, :, :], [128, 3, HT])
    names = ["gB", "gU", "gSL", "EGb", "DT", "Dm", "Nm", "qkDT", "qsT", "ks", "kbg", "vb", "u", "wT", "w", "vnew", "TT", "AT"]
    W = [{n: sb(f"dn_{n}{i}", [128, 128], F32) for n in names} for i in range(2)]
    MN = [[sb(f"dn_MN{i}_{k}", [128, 256], F32) for k in range(2)] for i in range(2)]
    for i in range(2):
        W[i]["Ne"] = sb(f"dn_Ne{i}", [128, 128], F32)
        W[i]["XT1"] = sb(f"dn_XT1{i}", [128, 256], F32)
    acur = [alias_base]

    def sba(name, shape):
        t = sb(name, shape, F32, at=acur[0])
        acur[0] += _prod(shape[1:]) * 4
        assert acur[0] <= alias_end
        return t

    W.append({n: sba(f"dn_{n}2", [128, 128]) for n in names})
    W[2]["Ne"] = sba("dn_Ne2", [128, 128])
    W[2]["XT1"] = sba("dn_XT12", [128, 256])
    MN.append([sba(f"dn_MN2_{k}", [128, 256]) for k in range(2)])
    Y = [[sb(f"dn_Y{i}_{k}", [128, 128], F32) for k in range(2)] for i in range(2)]
    Y.append([sba(f"dn_Y2_{k}", [128, 128]) for k in range(2)])
    S = L.S[h]

    def chunk_gen(c):
        cg = hh * 8 + c
        w = W[c % 3]
        t0 = c * 128
        qT = post[:, 0, t0:t0 + 128]
        kT = post[:, 1, t0:t0 + 128]
        vT = post[:, 2, t0:t0 + 128]
        col = lambda t: t[:, cg, h:h + 1]
        psA = K.ps_tile()
        P.tr(psA[:, 0:128], kT, C.ident[:])
        P.tr(psA[:, 128:256], vT, C.ident[:])
        yield
        P.copy("pool", w["gB"][:], L.g[:, cg, h:h + 1].to_broadcast([128, 128]))
        P.ts("dve", w["gU"][:], C.U[:], col(L.g), None, op0=ALU.mult)
        P.ts("dve", w["gSL"][:], C.SL[:], col(L.g), None, op0=ALU.mult)
        psB = K.ps_tile()
        P.mm(psB[:, 0:128], w["gB"][:], C.U[:])
        P.mm(psB[:, 128:256], w["gSL"][:], C.U[:], start=True, stop=False)
        P.mm(psB[:, 128:256], C.ident[:], C.negmT[:], start=False, stop=True)
        P.mm(psB[:, 256:384], w["gU"][:], C.SL[:], start=True, stop=False)
        P.mm(psB[:, 256:384], C.ident[:], C.posm[:], start=False, stop=True)
        P.act(w["EGb"][:], psB[:, 0:128], AF.Exp)
        P.act(w["DT"][:], psB[:, 128:256], AF.Exp)
        P.act(w["Dm"][:], psB[:, 256:384], AF.Exp)
        egl = w["EGb"][:, 127:128]
        yield
        psC = K.ps_tile()
        P.mm(psC[:, 0:128], kT, kT)
        P.mm(psC[:, 128:256], kT, qT)
        P.stt("dve", w["Nm"][:], psC[:, 0:128], col(L.nbeta), w["Dm"][:], ALU.mult, ALU.mult)
        P.tt("dve", w["qkDT"][:], psC[:, 128:256], w["DT"][:], ALU.mult)
        P.tt("pool", w["qsT"][:], qT, w["EGb"][:], ALU.mult)
        P.act(w["ks"][:], psA[:, 0:128], AF.Copy, scale=w["DT"][:, 127:128])
        P.act(w["kbg"][:], psA[:, 0:128], AF.Copy, scale=col(L.bg))
        P.act(w["vb"][:], psA[:, 128:256], AF.Copy, scale=col(L.beta))
        yield
        mn = MN[c % 3]
        yy = Y[c % 3]
        P.tt("pool", mn[0][:, 128:256], w["Nm"][:], C.bd16[:], ALU.mult)
        psD = K.ps_tile()
        P.tr(psD[:, 0:128], mn[0][:, 128:256], C.ident[:])
        P.copy("act", mn[0][:, 0:128], psD[:, 0:128])
        P.tt("dve", yy[0][:], psD[:, 0:128], C.ident[:], ALU.add)
        yield
        cur = 0
        ycur = 0
        for lev in range(3):
            Ma, Na = mn[cur][:, 0:128], mn[cur][:, 128:256]
            nxt = 1 - cur
            psE = K.ps_tile()
            if lev < 2:
                P.mm(psE[:, 0:128], Na, Ma)
                P.mm(psE[:, 128:256], Ma, Na)
                P.copy("act", mn[nxt][:, 0:256], psE[:, 0:256])
            else:
                P.mm(psE[:, 128:256], Ma, Na)
                P.copy("act", mn[nxt][:, 128:256], psE[:, 128:256])
            yield
            psF = K.ps_tile()
            P.mm(psF[:, 0:128], mn[nxt][:, 128:256], yy[ycur][:])
            P.tt("dve", yy[1 - ycur][:], yy[ycur][:], psF[:, 0:128], ALU.add)
            ycur = 1 - ycur
            cur = nxt
        for lev in range(3):
            yield
            P.tt("pool", w["Ne"][:], w["Nm"][:], C.em[lev][:], ALU.mult)
            psE = K.ps_tile()
            P.tr(psE[:, 0:128], yy[ycur][:], C.ident[:])
            P.mm(psE[:, 128:256], w["Ne"][:], yy[ycur][:])
            P.copy("act", w["XT1"][:, 0:256], psE[:, 0:256])
            yield
            psF = K.ps_tile()
            P.mm(psF[:, 0:128], w["XT1"][:, 0:128], w["XT1"][:, 128:256])
            P.tt("dve", yy[1 - ycur][:], yy[ycur][:], psF[:, 0:128], ALU.add)
            ycur = 1 - ycur
        XT = yy[ycur]
        yield
        psG = K.ps_tile()
        P.mm(psG[:, 0:128], XT[:], w["vb"][:])
        P.mm(psG[:, 128:256], w["kbg"][:], XT[:])
        P.mm(psG[:, 256:384], XT[:], w["kbg"][:])
        P.copy("act", w["u"][:], psG[:, 0:128])
        P.copy("act", w["wT"][:], psG[:, 128:256])
        P.copy("act", w["w"][:], psG[:, 256:384])
        yield
        psH = K.ps_tile()
        P.mm(psH[:, 0:128], w["wT"][:], S[:])
        P.tt("dve", w["vnew"][:], w["u"][:], psH[:, 0:128], ALU.subtract)
        psI = K.ps_tile()
        P.mm(psI[:, 0:128], S[:], w["qsT"][:], start=True, stop=False)
        P.mm(psI[:, 0:128], w["vnew"][:], w["qkDT"][:], start=False, stop=True)
        P.copy("act", ostage[:, t0:t0 + 128], psI[:, 0:128])
        psJ = K.ps_tile()
        P.mm(psJ[:, 0:128], w["ks"][:], w["vnew"][:])
        P.stt("dve", S[:], S[:], egl, psJ[:, 0:128], ALU.mult, ALU.add)
        pr = L.Pref[h][L.pref_i[h]]
        prn = L.Pref[h][1 - L.pref_i[h]]
        L.pref_i[h] = 1 - L.pref_i[h]
        psK = K.ps_tile()
        P.mm(psK[:, 0:128], w["w"][:], w["ks"][:])
        P.mm(psK[:, 128:256], w["w"][:], w["qkDT"][:])
        P.stt("dve", w["TT"][:], C.ident[:], egl, psK[:, 0:128], ALU.mult, ALU.subtract)
        P.tt("dve", w["AT"][:], w["qsT"][:], psK[:, 128:256], ALU.subtract)
        psL = K.ps_tile()
        P.mm(psL[:, 0:128], pr[:], w["AT"][:])
        P.mm(psL[:, 128:256], w["TT"][:], pr[:])
        P.copy("act", bstage[:, t0:t0 + 128], psL[:, 0:128])
        P.copy("act", prn[:], psL[:, 128:256])

    for a in range(0, 8, 3):
        active = [chunk_gen(c_) for c_ in range(a, min(a + 3, 8))]
        while active:
            for g_ in list(active):
                try:
                    next(g_)
                except StopIteration:
                    active.remove(g_)
    P.dma("sp", K.dn_o[h, :, hh * HT:(hh + 1) * HT], ostage[:, :], key=f"dn_o")
    P.dma("sp", K.dn_b[h, :, hh * HT:(hh + 1) * HT], bstage[:, :], key=f"dn_b")
    K.dump(f"dno{l}_{hh}_{h}", ostage[:, :], [128, HT])
    K.sb_cur = mark


def ret_head_unit(K, hT, hh, h, w3=None):
    P, sb, I, C, L = K.P, K.sb, K.I, K.C, K.L
    l = L.l
    HT = 1024
    mark = K.sb_cur
    if w3 is None:
        w3 = sb("rw3", [128, 3, 8, 128], BF16)
        for ch, c0 in enumerate((C_RQ, C_RK, C_RV)):
            load_w(K, w3[:, ch], I["w_in"][l], c0 + h * 128, 128, key=f"rw3_{ch}")
    pre = sb("rpre", [128, 2, HT], F32)
    rot = sb("rrot", [128, 2, HT], F32)
    ostage = sb("rostage", [128, HT], F32)
    qstage = sb("rqstage", [128, HT], F32)
    tcol0 = hh * HT
    for ch in range(2):
        for gi in range(2):
            ps = K.ps_tile()
            for kc in range(8):
                P.mm(ps[:, :], w3[:, ch, kc, :], hT[:, kc, HALO + gi * 512:HALO + (gi + 1) * 512],
                     start=(kc == 0), stop=(kc == 7))
            seg = pre[:, ch, gi * 512:(gi + 1) * 512]
            P.copy("act", seg, ps[:, :])
            ps2 = K.ps_tile()
            P.mm(ps2[:, :], C.Rm[:, :], seg)
            cs = K.cs_half[0][:, gi * 512:(gi + 1) * 512]
            sn = K.cs_half[1][:, gi * 512:(gi + 1) * 512]
            rseg = rot[:, ch, gi * 512:(gi + 1) * 512]
            P.tt("dve", rseg, ps2[:, :], sn, ALU.mult)
            P.tt("pool", seg, seg, cs, ALU.mult)
            P.tt("pool", seg, seg, rseg, ALU.add)
    K.dump(f"rqk{l}_{hh}_{h}", pre[:, :, :], [128, 2, HT])
    if DEBUG.get("ret_cut") == 0:
        K.dump(f"reto{l}_{hh}_{h}", pre[:, 0, :], [128, HT])
        K.sb_cur = mark
        return
    names = ["kz", "v", "scDT", "qx"]
    W = [{n: sb(f"rt_{n}{i}", [128, 128], F32) for n in names} for i in range(4)]
    Sr = L.Sr[h]
    g128 = math.exp(LOG_GAMMA[h] * 128.0)

    def chunk_gen(c):
        cg = hh * 8 + c
        w = W[c % 4]
        t0 = c * 128
        qT = pre[:, 0, t0:t0 + 128]
        kT = pre[:, 1, t0:t0 + 128]
        psA = K.ps_tile()
        P.tr(psA[:, 0:128], kT, C.ident[:])
        for kc in range(8):
            P.mm(psA[:, 128:256], hT[:, kc, HALO + t0:HALO + t0 + 128], w3[:, 2, kc, :],
                 start=(kc == 0), stop=(kc == 7))
        P.mm(psA[:, 256:384], kT, qT)
        yield
        P.act(w["kz"][:], psA[:, 0:128], AF.Copy, scale=C.zeta[:, h:h + 1])
        P.copy("act", w["v"][:], psA[:, 128:256])
        P.tt("dve", w["scDT"][:], psA[:, 256:384], C.decT[:, h, :], ALU.mult)
        P.tt("pool", w["qx"][:], qT, C.xi[:, h, :], ALU.mult)
        yield
        if DEBUG.get("ret_cut") == 1:
            P.copy("act", ostage[:, t0:t0 + 128], w["scDT"][:])
            return
        psB = K.ps_tile()
        P.mm(psB[:, 0:128], w["v"][:], w["scDT"][:], start=True, stop=False)
        P.mm(psB[:, 0:128], Sr[:], w["qx"][:], start=False, stop=True)
        P.copy("act", ostage[:, t0:t0 + 128], psB[:, 0:128])
        psC = K.ps_tile()
        P.mm(psC[:, 0:128], w["kz"][:], w["v"][:])
        P.stt("dve", Sr[:], Sr[:], g128, psC[:, 0:128], ALU.mult, ALU.add)
        P.ts("dve", qstage[:, t0:t0 + 128], w["qx"][:], math.exp(LOG_GAMMA[h] * 128.0 * cg), None, op0=ALU.mult)

    for a in range(0, 8, 4):
        active = [chunk_gen(c_) for c_ in range(a, a + 4)]
        while active:
            for g_ in list(active):
                try:
                    next(g_)
                except StopIteration:
                    active.remove(g_)
    P.dma("sp", K.ret_o[h, :, hh * HT:(hh + 1) * HT], ostage[:, :], key="ret_o")
    P.dma("sp", K.ret_q[h, :, hh * HT:(hh + 1) * HT], qstage[:, :], key="ret_q")
    K.dump(f"reto{l}_{hh}_{h}", ostage[:, :], [128, HT])
    K.sb_cur = mark


def exchange_states(K):
    P, sb, I, C, L = K.P, K.sb, K.I, K.C, K.L
    l = L.l
    mark = K.sb_cur
    stout = sb("stout", [128, 12, 128], F32)
    ps = K.ps_tile()
    for h in range(NH):
        P.tr(ps[:, h * 128:(h + 1) * 128], L.Pref[h][L.pref_i[h]][:], C.ident[:])
    P.copy("act", stout[:, 0:4, :], ps[:, :].rearrange("p (h n) -> p h n", n=128))
    for h in range(NH):
        P.copy("pool", stout[:, 4 + h, :], L.S[h][:])
        P.copy("pool", stout[:, 8 + h, :], L.Sr[h][:])
    P.dma("sp", K.st_src.rearrange("(i p) n -> p i n", p=128), stout[:, :, :], key="st_src")
    P.allgather(K.st_all, K.st_src, GROUPS, key=f"ag_st{l}")
    K.sb_cur = mark


def exchange_finish(K):
    P, sb, I, C, L = K.P, K.sb, K.I, K.C, K.L
    l = L.l
    mark = K.sb_cur
    stin = sb("stin", [128, 3, 12, 128], F32)
    src = K.st_all.rearrange("(r i p) n -> p r i n", r=RPB, i=12, p=128)
    for r in range(3):
        P.dma("sp", stin[:, r], src[:, r], key=f"stin{r}")
    mr = sb("mr", [128, 4], F32)
    aa = sb("aa", [128, 4], F32)
    cf = sb("cf", [128, 16], F32)
    for r in range(3):
        P.ts("dve", mr[:, r:r + 1], K.meta[:, 0:1], float(r), None, op0=ALU.is_gt)
        P.ts("dve", aa[:, r:r + 1], K.meta[:, 0:1], -(1.0 + r), 0.0, op0=ALU.add, op1=ALU.max)
        for h in range(NH):
            P.act(cf[:, r * 4 + h:r * 4 + h + 1], aa[:, r:r + 1], AF.Exp, scale=LOG_GAMMA[h] * float(T))
        P.ts("dve", cf[:, r * 4:r * 4 + 4], cf[:, r * 4:r * 4 + 4], mr[:, r:r + 1], None, op0=ALU.mult)
    P.memset("pool", L.Sin[:], 0.0)
    tt_ = [sb(f"ex_t{i}", [128, 128], F32) for i in range(2)]
    for h in range(NH):
        for r in range(3):
            t = tt_[(h * 3 + r) % 2]
            ps = K.ps_tile()
            P.mm(ps[:, 0:128], stin[:, r, h, :], L.Sin[:, h, :])
            P.tt("dve", t[:], ps[:, 0:128], stin[:, r, 4 + h, :], ALU.add)
            P.tt("pool", t[:], t[:], L.Sin[:, h, :], ALU.subtract)
            P.stt("dve", L.Sin[:, h, :], t[:], mr[:, r:r + 1], L.Sin[:, h, :], ALU.mult, ALU.add)
        P.ts("dve", L.Srin[:, h, :], stin[:, 0, 8 + h, :], cf[:, h:h + 1], None, op0=ALU.mult)
        for r in (1, 2):
            P.stt("dve", L.Srin[:, h, :], stin[:, r, 8 + h, :], cf[:, r * 4 + h:r * 4 + h + 1], L.Srin[:, h, :],
                  ALU.mult, ALU.add)
    K.dump(f"Sin{l}", L.Sin[:], [128, NH, 128])
    K.dump(f"Srin{l}", L.Srin[:], [128, NH, 128])
    K.sb_cur = mark


def inproj_fm(K, hT, wt, n0, n, dst_fn):
    P = K.P
    c = 0
    while c < n:
        m = min(512, n - c)
        ps = K.ps_tile()
        for kc in range(8):
            P.mm(ps[:, 0:m], wt[:, kc, :], hT[:, kc, n0 + c:n0 + c + m], start=(kc == 0), stop=(kc == 7))
        dst_fn(ps, c, m)
        c += m


def phase_c_branches(K, hT, hh, yT):
    P, sb, I, C, L = K.P, K.sb, K.I, K.C, K.L
    l = L.l
    HT = 1024
    c0 = hh * HT

    def run_rr(gens):
        active = list(gens)
        while active:
            for g_ in list(active):
                try:
                    next(g_)
                except StopIteration:
                    active.remove(g_)

    mark = K.sb_cur
    pre = sb("b_pre", [128, HALO + HT], F32)
    sA = sb("b_sA", [128, HALO + HT], F32)
    sB = sb("b_sB", [128, HALO + HT], F32)
    pooled = sb("b_pooled", [128, HT], BF16)
    wp = [sb(f"b_wp{i}", [128, 8, 128], BF16) for i in range(2)]
    wpool = sb("b_wpool", [128, NH, 128], BF16)
    P.dma("pool", wpool[:, :, :], I["pool_w"][l].rearrange("g c d -> c g d"), key="b_wpool")
    N = HALO + HT
    wsv = sb("s_wsv", [128, 8, 512], BF16)
    load_w(K, wsv, I["w_in"][l], C_SV, 512, key="s_wsv")
    wsu = [sb(f"s_wsu{i}", [128, 8, 128], BF16) for i in range(2)]
    uT = sb("s_uT", [128, NH, HT], BF16)
    gv = [sb(f"s_gv{i}", [128, 512], F32) for i in range(2)]
    junk = sb("s_junk", [128, 512], F32)
    vt = [sb(f"s_vt{i}", [128, 512], BF16) for i in range(2)]
    st = [sb(f"s_st{i}", [128, 8], F32) for i in range(2)]
    def gen_B():
        for g in range(NH):
            win = 2 << g
            load_w(K, wp[g % 2], I["w_in"][l], C_PL + g * 128, 128, key=f"b_wp{g % 2}")
            if hh == 0:
                ps = K.ps_tile()
                for kc in range(8):
                    P.mm(ps[:, 0:HALO], wp[g % 2][:, kc, :], hT[:, kc, 0:HALO], start=(kc == 0), stop=(kc == 7))
                P.ts("dve", pre[:, 0:HALO], ps[:, 0:HALO], K.maskj[:, 0:1], None, op0=ALU.mult)
            else:
                P.copy("pool", pre[:, 0:HALO], L.pcarry[:, g, :])
            inproj_fm(K, hT, wp[g % 2], HALO, HT, lambda ps, c, m: P.copy("act", pre[:, HALO + c:HALO + c + m], ps[:, 0:m]))
            if hh == 0:
                P.copy("pool", L.pcarry[:, g, :], pre[:, HT:HT + HALO])
            yield
            src, sh, bufs, k = pre, 1, [sA, sB], 0
            while sh < win:
                dst = bufs[k % 2]
                lo = 2 * sh - 1
                P.tt("pool" if k % 2 else "dve", dst[:, lo:N], src[:, lo:N], src[:, lo - sh:N - sh], ALU.add)
                src, sh, k = dst, sh * 2, k + 1
            tmpb = bufs[k % 2]
            P.stt("dve", tmpb[:, HALO:N], src[:, HALO:N], 1.0 / win, pre[:, HALO:N], ALU.mult, ALU.subtract)
            if hh == 0:
                P.tt("dve", tmpb[:, HALO:2 * HALO], src[:, HALO:2 * HALO], K.invc[:, g, :], ALU.mult)
                P.tt("dve", tmpb[:, HALO:2 * HALO], tmpb[:, HALO:2 * HALO], pre[:, HALO:2 * HALO], ALU.subtract)
            yield
            P.copy("pool", pooled[:, :], tmpb[:, HALO:N])
            for gi in range(2):
                cs = slice(gi * 512, (gi + 1) * 512)
                ps = K.ps_tile()
                P.mm(ps[:, :], wpool[:, g, :], pooled[:, cs])
                P.act(yT[:, 4 + g, cs], ps[:, :], AF.Copy, scale=K.VT[l][:, 113 + g:114 + g])
            yield
    def gen_C():
        for g in range(NH):
            load_w(K, wsu[g % 2], I["w_in"][l], C_SU + g * 128, 128, key=f"s_wsu{g % 2}")
            inproj_fm(K, hT, wsu[g % 2], HALO, HT,
                      lambda ps, c, m, g=g: P.act(uT[:, g, c:c + m], ps[:, 0:m], AF.Gelu))
            yield
        for tt in range(8):
            g_ = gv[tt % 2]
            s_ = st[tt % 2]
            ps = K.ps_tile()
            for kc in range(8):
                P.mm(ps[:, :], hT[:, kc, HALO + tt * 128:HALO + (tt + 1) * 128], wsv[:, kc, :],
                     start=(kc == 0), stop=(kc == 7))
            P.act(g_[:, :], ps[:, :], AF.Gelu, accum_out=s_[:, 0:1])
            P.act(junk[:, :], g_[:, :], AF.Square, accum_out=s_[:, 1:2])
            yield
            P.ts("dve", s_[:, 2:3], s_[:, 0:1], 1.0 / 512.0, None, op0=ALU.mult)
            P.tt("dve", s_[:, 3:4], s_[:, 2:3], s_[:, 2:3], ALU.mult)
            P.stt("dve", s_[:, 4:5], s_[:, 1:2], 1.0 / 512.0, s_[:, 3:4], ALU.mult, ALU.subtract)
            rsqrt(K, s_[:, 5:6], s_[:, 4:5], 1.0, EPS)
            P.ts("dve", g_[:, :], g_[:, :], s_[:, 2:3], s_[:, 5:6], op0=ALU.subtract, op1=ALU.mult)
            P.tt("pool", g_[:, :], g_[:, :], L.lng[:, :], ALU.mult)
            P.tt("pool", vt[tt % 2][:, :], g_[:, :], L.lnb[:, :], ALU.add)
            yield
            ps2 = K.ps_tile()
            for g in range(NH):
                P.mm(ps2[:, g * 128:(g + 1) * 128], vt[tt % 2][:, g * 128:(g + 1) * 128], L.WcT[:, g, :])
            P.tt("dve", junk[:, :], ps2[:, :], L.sgb[:, :], ALU.add)
            P.tt("pool", yT[:, 8:12, tt * 128:(tt + 1) * 128], junk[:, :].rearrange("p (g n) -> p g n", n=128),
                 uT[:, :, tt * 128:(tt + 1) * 128], ALU.mult)


    run_rr([gen_B(), gen_C()])
    K.sb_cur = mark
    if hh == 0:
        exchange_finish(K)
    mark = K.sb_cur
    o_sb = sb("a_o", [128, HT], F32)
    b_sb = sb("a_b", [128, HT], F32)
    zs = [sb(f"a_zs{i}", [128, 512], F32) for i in range(2)]
    wz = [sb(f"a_wz{i}", [128, 8, 128], BF16) for i in range(2)]
    sqA = [sb(f"a_sq{i}", [128, 512], F32) for i in range(2)]
    o_sbD = sb("d_o", [128, HT], F32)
    q_sb = sb("d_q", [128, HT], F32)
    zsD = [sb(f"d_zsD{i}", [128, 512], F32) for i in range(2)]
    mt = [sb(f"d_m{i}", [128, 512], F32) for i in range(2)]
    wzD = [sb(f"d_wzD{i}", [128, 8, 128], BF16) for i in range(2)]
    sqD = [sb(f"d_sq{i}", [128, 512], F32) for i in range(2)]
    def gen_A():
        for h in range(NH):
            load_w(K, wz[h % 2], I["w_in"][l], C_DZ + h * 128, 128, key=f"a_wz{h % 2}")
            P.dma("sp", o_sb[:, :], K.dn_o[h, :, c0:c0 + HT], key="a_o")
            P.dma("sp", b_sb[:, :], K.dn_b[h, :, c0:c0 + HT], key="a_b")
            for gi in range(2):
                cs = slice(gi * 512, (gi + 1) * 512)
                ps = K.ps_tile()
                P.mm(ps[:, :], L.Sin[:, h, :], b_sb[:, cs])
                P.tt("dve", o_sb[:, cs], o_sb[:, cs], ps[:, :], ALU.add)
                yield
                sq = sqA[gi % 2]
                P.act(sq[:, :], o_sb[:, cs], AF.Square)
                ps2 = K.ps_tile()
                P.mm(ps2[:, :], C.ones[:, :], sq[:, :])
                rsqrt(K, sq[:, :], ps2[:, :], 1.0 / 128.0, EPS)
                P.stt("dve", o_sb[:, cs], o_sb[:, cs], K.VT[l][:, 112:113], sq[:, :], ALU.mult, ALU.mult)
                yield
                z = zs[gi % 2]
                ps3 = K.ps_tile()
                for kc in range(8):
                    P.mm(ps3[:, :], wz[h % 2][:, kc, :], hT[:, kc, HALO + gi * 512:HALO + (gi + 1) * 512],
                         start=(kc == 0), stop=(kc == 7))
                P.act(z[:, :], ps3[:, :], AF.Silu)
                P.tt("pool", yT[:, 0 + h, cs], o_sb[:, cs], z[:, :], ALU.mult)
                yield
    def gen_D():
        for h in range(NH):
            load_w(K, wzD[h % 2], I["w_in"][l], C_RG + h * 128, 128, key=f"d_wzD{h % 2}")
            P.dma("sp", o_sbD[:, :], K.ret_o[h, :, c0:c0 + HT], key="d_o")
            P.dma("sp", q_sb[:, :], K.ret_q[h, :, c0:c0 + HT], key="d_q")
            for gi in range(2):
                cs = slice(gi * 512, (gi + 1) * 512)
                ps = K.ps_tile()
                P.mm(ps[:, :], L.Srin[:, h, :], q_sb[:, cs])
                P.tt("dve", o_sbD[:, cs], o_sbD[:, cs], ps[:, :], ALU.add)
                yield
                sq = sqD[gi % 2]
                m = mt[gi % 2]
                P.act(sq[:, :], o_sbD[:, cs], AF.Square)
                ps1 = K.ps_tile()
                P.mm(ps1[:, :], C.ones[:, :], o_sbD[:, cs])
                ps2 = K.ps_tile()
                P.mm(ps2[:, :], C.ones[:, :], sq[:, :])
                yield
                P.ts("dve", m[:, :], ps1[:, :], 1.0 / 128.0, None, op0=ALU.mult)
                P.tt("pool", sq[:, :], m[:, :], m[:, :], ALU.mult)
                P.stt("dve", sq[:, :], ps2[:, :], 1.0 / 128.0, sq[:, :], ALU.mult, ALU.subtract)
                rsqrt(K, sq[:, :], sq[:, :], 1.0, EPS)
                P.tt("pool", o_sbD[:, cs], o_sbD[:, cs], m[:, :], ALU.subtract)
                P.stt("dve", o_sbD[:, cs], o_sbD[:, cs], K.VT[l][:, 117 + h:118 + h], sq[:, :], ALU.mult, ALU.mult)
                yield
                z = zsD[gi % 2]
                ps3 = K.ps_tile()
                for kc in range(8):
                    P.mm(ps3[:, :], wzD[h % 2][:, kc, :], hT[:, kc, HALO + gi * 512:HALO + (gi + 1) * 512],
                         start=(kc == 0), stop=(kc == 7))
                P.act(z[:, :], ps3[:, :], AF.Silu)
                P.tt("pool", yT[:, 12 + h, cs], o_sbD[:, cs], z[:, :], ALU.mult)
                yield
    run_rr([gen_A(), gen_D()])
    K.sb_cur = mark


def phase_c_merge(K, hT, hh, yT):
    P, sb, I, C, L = K.P, K.sb, K.I, K.C, K.L
    l = L.l
    HT = 1024
    mark = K.sb_cur
    merged = sb("m_merged", [128, 8, HT], BF16)
    wg = [sb(f"m_wg{i}", [128, 4, 8, 128], BF16) for i in range(2)]
    wb = [sb(f"m_wb{i}", [128, 4, 4, 128], BF16) for i in range(2)]
    sg = [sb(f"m_sg{i}", [128, 512], F32) for i in range(2)]
    acc = [sb(f"m_acc{i}", [128, 512], F32) for i in range(2)]
    tmp = [sb(f"m_tmp{i}", [128, 512], F32) for i in range(2)]
    brw = ("w_br_dn", "w_br_pool", "w_br_sg", "w_br_ret")
    n = 0

    def ld_m(buf, bi, dc):
        for br in range(4):
            load_w(K, wg[bi][:, br], I["w_in"][l], C_GATES + br * D + dc * 128, 128, key=f"m_wg{bi}_{br}")
            load_w(K, wb[bi][:, br], I[brw[br]][l], dc * 128, 128, key=f"m_wb{bi}_{br}")

    wsm = WStream([0, 1], [(lambda buf, bi, dc=dc: ld_m(buf, bi, dc)) for dc in range(8)], 1)
    wo = [sb(f"m_wo{i}", [128, 8, 128], BF16) for i in range(2)]
    wso = WStream(wo, [(lambda buf, bi, dc=dc: load_w(K, buf, I["w_out"][l], dc * 128, 128, key=f"m_wo{bi}"))
                       for dc in range(8)], 1)
    for dc in range(8):
        wsm.get(dc)
        for gi in range(2):
            cs = slice(gi * 512, (gi + 1) * 512)
            a = acc[gi % 2]
            for br in range(4):
                s_ = sg[n % 2]
                t_ = tmp[n % 2]
                n += 1
                ps = K.ps_tile()
                for kc in range(8):
                    P.mm(ps[:, :], wg[dc % 2][:, br, kc, :], hT[:, kc, HALO + gi * 512:HALO + (gi + 1) * 512],
                         start=(kc == 0), stop=(kc == 7))
                P.act(s_[:, :], ps[:, :], AF.Sigmoid)
                ps2 = K.ps_tile()
                for kc in range(4):
                    P.mm(ps2[:, :], wb[dc % 2][:, br, kc, :], yT[:, br * 4 + kc, cs], start=(kc == 0), stop=(kc == 3))
                if br == 0:
                    P.tt("dve", a[:, :], s_[:, :], ps2[:, :], ALU.mult)
                elif br < 3:
                    P.tt("dve", t_[:, :], s_[:, :], ps2[:, :], ALU.mult)
                    P.tt("dve", a[:, :], a[:, :], t_[:, :], ALU.add)
                else:
                    P.tt("dve", t_[:, :], s_[:, :], ps2[:, :], ALU.mult)
                    P.tt("pool", merged[:, dc, cs], a[:, :], t_[:, :], ALU.add)
    K.dump(f"merged{l}_{hh}", merged[:, :, :], [128, 8, HT], BF16)
    G1 = K.modT[l][:, 16:24]
    for dc in range(8):
        wo = {dc % 2: wso.get(dc)}
        for gi in range(2):
            cs = slice(gi * 512, (gi + 1) * 512)
            xs = K.xT[:, dc, HALO + hh * HT + gi * 512:HALO + hh * HT + (gi + 1) * 512]
            ps = K.ps_tile()
            for kc in range(8):
                P.mm(ps[:, :], wo[dc % 2][:, kc, :], merged[:, kc, cs], start=(kc == 0), stop=(kc == 7))
            P.stt("dve", xs, ps[:, :], G1[:, dc:dc + 1], xs, ALU.mult, ALU.add)
    K.sb_cur = mark


def mlp_half(K, l, hh, hT):
    P, sb, I, C = K.P, K.sb, K.I, K.C
    HT = 1024
    A2, B2, G2 = K.AB[l][:, 16:24], K.modT[l][:, 24:32], K.modT[l][:, 40:48]
    norm_h(K, A2, B2, HALO + hh * HT, HT, hT, HALO)
    mark = K.sb_cur
    uT = sb("f_uT", [128, 32, HT], BF16)
    w1b = [sb(f"f_w1{i}", [128, 8, 128], BF16) for i in range(4)]
    rl = [sb(f"f_rl{i}", [128, 512], BF16) for i in range(2)]
    n = 0
    ws1 = WStream(w1b, [(lambda buf, bi, fc=fc: load_w(K, buf, I["mlp_w1"][l], fc * 128, 128, key=f"f_w1{bi}"))
                        for fc in range(32)], 3)
    w2 = [sb(f"f_w2{i}", [128, 32, 128], BF16) for i in range(2)]
    ws2 = WStream(w2, [(lambda buf, bi, dc=dc: load_w(K, buf, I["mlp_w2"][l], dc * 128, 128, key=f"f_w2{bi}"))
                       for dc in range(8)], 1)
    for fc in range(32):
        w1 = {fc % 2: ws1.get(fc)}
        for gi in range(2):
            cs = slice(gi * 512, (gi + 1) * 512)
            ps = K.ps_tile()
            for kc in range(8):
                P.mm(ps[:, :], w1[fc % 2][:, kc, :], hT[:, kc, HALO + gi * 512:HALO + (gi + 1) * 512],
                     start=(kc == 0), stop=(kc == 7))
            r = rl[n % 2]
            n += 1
            P.act(r[:, :], ps[:, :], AF.Relu)
            P.tt("pool", uT[:, fc, cs], r[:, :], r[:, :], ALU.mult)
    for dc in range(8):
        w2 = {dc % 2: ws2.get(dc)}
        for gi in range(2):
            cs = slice(gi * 512, (gi + 1) * 512)
            xs = K.xT[:, dc, HALO + hh * HT + gi * 512:HALO + hh * HT + (gi + 1) * 512]
            ps = K.ps_tile()
            for kc in range(32):
                P.mm(ps[:, :], w2[dc % 2][:, kc, :], uT[:, kc, cs], start=(kc == 0), stop=(kc == 31))
            P.stt("dve", xs, ps[:, :], G2[:, dc:dc + 1], xs, ALU.mult, ALU.add)
    K.sb_cur = mark


def exchange_xhalo(K):
    P, sb, I, C = K.P, K.sb, K.I, K.C
    mark = K.sb_cur
    st = sb("xh_st", [128, 8, HALO], F32)
    P.copy("pool", st[:, :, :], K.xT[:, :, T:T + HALO])
    P.dma("sp", K.xh_src.rearrange("(fc p) n -> p fc n", p=128), st[:, :, :], key="xh_src")
    P.allgather(K.xh_all, K.xh_src, GROUPS, key="ag_xh")
    xin = sb("xh_in", [128, 3, 8, HALO], F32)
    src = K.xh_all.rearrange("(r fc p) n -> p r fc n", r=RPB, fc=8, p=128)
    sel = sb("xh_sel", [128, 4], F32)
    for r in range(3):
        P.dma("sp", xin[:, r], src[:, r], key=f"xh_in{r}")
        P.ts("dve", sel[:, r:r + 1], K.meta[:, 0:1], float(r + 1), None, op0=ALU.is_equal)
    P.ts("dve", K.xT[:, :, 0:HALO], xin[:, 0], sel[:, 0:1], None, op0=ALU.mult)
    for r in (1, 2):
        P.stt("dve", K.xT[:, :, 0:HALO], xin[:, r], sel[:, r:r + 1], K.xT[:, :, 0:HALO], ALU.mult, ALU.add)
    K.sb_cur = mark


def final_norm(K):
    P, sb, I, C = K.P, K.sb, K.I, K.C
    mark = K.sb_cur
    sq = [sb(f"fn_sq{i}", [128, 512], F32) for i in range(2)]
    rs = sb("fn_rs", [128, 512], F32)
    yT = sb("fn_y", [128, 8, 512], F32)
    ot = [sb(f"fn_o{i}", [128, D], F32) for i in range(2)]
    for gi in range(T // 512):
        c0 = HALO + gi * 512
        ps = K.ps_tile()
        for fc in range(8):
            q = sq[fc % 2]
            P.act(q[:, :], K.xT[:, fc, c0:c0 + 512], AF.Square)
            P.mm(ps[:, :], C.ones[:, :], q[:, :], start=(fc == 0), stop=(fc == 7))
        rsqrt(K, rs[:, :], ps[:, :], 1.0 / D, EPS)
        for fc in range(8):
            P.stt("dve", yT[:, fc, :], K.xT[:, fc, c0:c0 + 512], K.fgT[:, fc:fc + 1], rs[:, :], ALU.mult, ALU.mult)
        for t4 in range(4):
            tt = gi * 4 + t4
            o = ot[tt % 2]
            for half in range(2):
                ps2 = K.ps_tile()
                for q4 in range(4):
                    fc = half * 4 + q4
                    P.tr(ps2[:, q4 * 128:(q4 + 1) * 128], yT[:, fc, t4 * 128:(t4 + 1) * 128], C.ident[:])
                P.copy("act", o[:, half * 512:(half + 1) * 512], ps2[:, :])
            P.dma("sp", K.out[tt * 128:(tt + 1) * 128, :], o[:, :], key=f"out{tt % 2}")
    K.sb_cur = mark


def layer(K, l, stop_after):
    P, sb, I, C = K.P, K.sb, K.I, K.C
    mark = K.sb_cur
    K.sq = [sb(f"sq{i}", [128, 512], F32) for i in range(2)]
    L = Ctx()
    K.L = L
    L.l = l
    mark_c = K.sb_cur
    layer_setup_c(K, l)
    mark_a = K.sb_cur
    layer_setup(K, l)
    A1, B1 = K.AB[l][:, 0:8], K.modT[l][:, 0:8]
    HT = 1024
    for hh in range(2):
        m2 = K.sb_cur
        hT = sb("hT", [128, 8, HALO + HT], BF16)
        K.cs_half = [sb(f"cs_half{i}", [128, HT], F32) for i in range(2)]
        for i in range(2):
            P.dma("sp", K.cs_half[i][:, :], K.cs_dram[i, :, hh * HT:(hh + 1) * HT], key=f"cs_half{i}")
        if hh == 0:
            norm_h(K, A1, B1, 0, HALO, hT, 0)
        norm_h(K, A1, B1, HALO + hh * HT, HT, hT, HALO)
        K.dump(f"hT{l}_{hh}", hT[:, :, :], [128, 8, HALO + HT], BF16)
        if stop_after == f"L{l}h{hh}norm":
            return
        bg_unit(K, hT, hh)
        for h in range(NH):
            if stop_after == f"L{l}h{hh}ret{h}":
                ret_head_unit(K, hT, hh, h)
                return
            dn_head_unit(K, hT, hh, h)
            if stop_after == f"L{l}h{hh}dn{h}":
                return
        rwb = [sb(f"rw3b{i}", [128, 3, 8, 128], BF16) for i in range(2)]

        def ld_r(buf, bi, h_):
            for ch, c0 in enumerate((C_RQ, C_RK, C_RV)):
                load_w(K, buf[:, ch], I["w_in"][l], c0 + h_ * 128, 128, key=f"rw3b{bi}_{ch}")

        wsr = WStream(rwb, [(lambda buf, bi, h_=h_: ld_r(buf, bi, h_)) for h_ in range(NH)], 1)
        for h in range(NH):
            ret_head_unit(K, hT, hh, h, w3=wsr.get(h))
        K.sb_cur = m2
    if stop_after == f"L{l}A":
        return
    exchange_states(K)
    if stop_after == f"L{l}X":
        return
    K.sb_cur = mark_a
    for hh in range(2):
        m2 = K.sb_cur
        hT = sb("hTc", [128, 8, HALO + HT], BF16)
        if hh == 0:
            norm_h(K, A1, B1, 0, HALO, hT, 0, compute_rstd=False)
        norm_h(K, A1, B1, HALO + hh * HT, HT, hT, HALO, compute_rstd=False)
        yT = sb("yT", [128, 16, HT], BF16)
        phase_c_branches(K, hT, hh, yT)
        K.dump(f"yT{l}_{hh}", yT[:, :, :], [128, 16, HT], BF16)
        if stop_after == f"L{l}C{hh}br":
            return
        phase_c_merge(K, hT, hh, yT)
        if stop_after == f"L{l}C{hh}":
            K.dump(f"xmid{l}_{hh}", K.xT[:, :, :], [128, 8, HALO + T])
            return
        K.sb_cur = m2
    K.sb_cur = mark_c
    for hh in range(2):
        m2 = K.sb_cur
        hT = sb("hTm", [128, 8, HALO + HT], BF16)
        mlp_half(K, l, hh, hT)
        K.sb_cur = m2
    K.dump(f"xout{l}", K.xT[:, :, :], [128, 8, HALO + T])
    if l < DEPTH - 1:
        exchange_xhalo(K)
    K.sb_cur = mark


_CACHE = {}


def make_in_maps(inputs):
    x = np.ascontiguousarray(inputs["x"], dtype=np.float32)
    c = np.asarray(inputs["c"], dtype=np.float32)
    pos = np.asarray(inputs["positions"], dtype=np.int32)
    shared = {}
    for k in ("norm1_g", "norm2_g", "ada_w", "ada_b", "w_in", "dn_conv_w", "dn_a_log", "dn_dt_bias",
              "dn_norm_g", "pool_w", "pool_scale", "sg_ln_g", "sg_ln_b", "sg_w", "sg_b", "ret_gn_g",
              "w_br_dn", "w_br_pool", "w_br_sg", "w_br_ret", "w_out", "mlp_w1", "mlp_w2"):
        shared[k] = np.ascontiguousarray(inputs[k], dtype=np.float32)
    shared["final_g"] = np.ascontiguousarray(inputs["final_g"], dtype=np.float32).reshape(1, D)
    maps = []
    for core in range(NCORE):
        b, j = divmod(core, RPB)
        m = dict(shared)
        m["x"] = np.ascontiguousarray(x[b, j * T:(j + 1) * T])
        if j == 0:
            m["xh"] = np.zeros((HALO, D), np.float32)
        else:
            m["xh"] = np.ascontiguousarray(x[b, j * T - HALO:j * T])
        m["cT"] = np.ascontiguousarray(c[b].reshape(8, 128).T)
        m["pos"] = np.ascontiguousarray(pos[b, j * T:(j + 1) * T].reshape(1, T))
        meta = np.zeros((128, 8), np.float32)
        meta[:, 0] = j
        meta[:, 1] = j * T
        invf = (10000.0 ** (-np.arange(0, 128, 2, dtype=np.float32) / np.float32(128))).astype(np.float32)
        meta[:, 2] = np.concatenate([invf, invf])
        m["meta"] = meta
        maps.append(m)
    return maps


def kernel(**inputs):
    if "nc" not in _CACHE:
        _CACHE["nc"] = build()[0]
    nc = _CACHE["nc"]
    maps = make_in_maps(inputs)
    res = run_bass_kernel_spmd(nc, maps, core_ids=list(range(NCORE)))
    out = np.empty((NB, SEQ, D), np.float32)
    for core in range(NCORE):
        b, j = divmod(core, RPB)
        out[b, j * T:(j + 1) * T] = res.results[core]["out"]
    return out
```

```python
import math
import numpy as np
import concourse.bass as bass
import concourse.mybir as mybir
from concourse.bass_utils import run_bass_kernel_spmd

F32 = mybir.dt.float32
BF16 = mybir.dt.bfloat16
I32 = mybir.dt.int32
AF = mybir.ActivationFunctionType
ALU = mybir.AluOpType
AX = mybir.AxisListType
ESZ = {F32: 4, BF16: 2, I32: 4}

ENGS = ("pe", "act", "dve", "pool", "sp")


class Op:
    __slots__ = ("eng", "emit", "deps", "signal", "count", "is_dma", "sem", "semval", "idx", "inc")

    def __init__(self, eng, emit, is_dma=False):
        self.eng = eng
        self.emit = emit
        self.deps = ()
        self.signal = False
        self.count = 0
        self.is_dma = is_dma
        self.sem = None
        self.semval = 0
        self.inc = 16


class Prog:
    def __init__(self, nc):
        self.nc = nc
        self.ops = []
        self.tinfo = {}
        self.hist = {}
        self.dma_keys = {}
        self.sem_cm = []
        self.single = False
        self.n_sems = 0

    def reg(self, handle, space, base, pbytes):
        self.tinfo[handle.name] = (space, base, pbytes)
        return handle

    def dram(self, name, shape, dtype, kind):
        t = self.nc.dram_tensor(name, list(shape), dtype, kind=kind)
        n = 1
        for s in shape:
            n *= s
        self.tinfo[t.name] = ("D:" + t.name, 0, n * ESZ[dtype])
        return t.ap()

    def region(self, ap):
        space, base, pbytes = self.tinfo[ap.tensor.name]
        esz = ESZ[ap.dtype]
        pairs = ap.ap
        off = ap.offset
        if space[0] == "D":
            ext = 1
            for s, c in pairs:
                ext += (c - 1) * abs(s)
            return (space, 0, 1, off * esz, (off + ext) * esz)
        ps = pbytes // esz
        p0 = off // ps
        f0 = off % ps
        s0, c0 = pairs[0]
        ext = 1
        for s, c in pairs[1:]:
            ext += (c - 1) * abs(s)
        if s0 != ps and c0 != 1:
            ext += (c0 - 1) * abs(s0)
            c0 = 1
        lo, hi = base + f0 * esz, base + (f0 + ext) * esz
        if space == "PS":
            lo = lo // 2048 * 2048
            hi = (hi + 2047) // 2048 * 2048
            return (space, 0, 128, lo, hi)
        return (space, p0, p0 + c0, lo, hi)

    def add(self, eng, emit, reads, writes, is_dma=False, key=None, inc=16):
        op = Op(eng, emit, is_dma)
        idx = len(self.ops)
        op.idx = idx
        self.ops.append(op)
        deps = set()
        rregs = [self.region(a) for a in reads]
        wregs = [self.region(a) for a in writes]
        for (sp, p0, p1, b0, b1) in rregs:
            for rec in self.hist.get(sp, ()):
                if rec[0] < p1 and p0 < rec[1] and rec[2] < b1 and b0 < rec[3]:
                    if rec[5] or (sp == "PS" and self.ops[rec[4]].eng != eng):
                        deps.add(rec[4])
        for (sp, p0, p1, b0, b1) in wregs:
            for rec in self.hist.get(sp, ()):
                if rec[0] < p1 and p0 < rec[1] and rec[2] < b1 and b0 < rec[3]:
                    deps.add(rec[4])
        for (sp, p0, p1, b0, b1) in wregs:
            lst = self.hist.setdefault(sp, [])
            lst[:] = [r for r in lst if not (p0 <= r[0] and r[1] <= p1 and b0 <= r[2] and r[3] <= b1)]
            lst.append([p0, p1, b0, b1, idx, True])
        for (sp, p0, p1, b0, b1) in rregs:
            lst = self.hist.setdefault(sp, [])
            if not is_dma:
                lst[:] = [r for r in lst if not ((not r[5]) and r[0] == p0 and r[1] == p1 and r[2] == b0
                                                 and r[3] == b1 and self.ops[r[4]].eng == eng
                                                 and not self.ops[r[4]].is_dma)]
            lst.append([p0, p1, b0, b1, idx, False])
        best = {}
        keep = []
        for d in deps:
            A = self.ops[d]
            if A.is_dma:
                keep.append(d)
            else:
                if A.eng == "pe" and eng == "pe" and not is_dma:
                    continue
                if A.eng not in best or best[A.eng] < d:
                    best[A.eng] = d
        keep.extend(best.values())
        for d in keep:
            self.ops[d].signal = True
        op.deps = tuple(sorted(keep))
        if is_dma:
            if key not in self.dma_keys:
                self.dma_keys[key] = [None, 0]
            ent = self.dma_keys[key]
            ent[1] += inc
            op.sem = key
            op.semval = ent[1]
            op.inc = inc
        return op

    def mm(self, out, lhsT, rhs, start=True, stop=True):
        return self.add("pe", lambda e: e.matmul(out, lhsT, rhs, start=start, stop=stop),
                        [lhsT, rhs], [out])

    def tr(self, out, in_, ident):
        return self.add("pe", lambda e: e.transpose(out, in_, ident), [in_, ident], [out])

    def act(self, out, in_, func, bias=None, scale=1.0, accum_out=None, eng="act"):
        reads = [in_]
        kw = {}
        if bias is not None:
            kw["bias"] = bias
            if not isinstance(bias, (int, float)):
                reads.append(bias)
        if not isinstance(scale, (int, float)):
            reads.append(scale)
        kw["scale"] = scale
        writes = [out]
        if accum_out is not None:
            kw["accum_out"] = accum_out
            writes.append(accum_out)
        return self.add("act", lambda e: e.activation(out, in_, func, **kw), reads, writes)

    def tt(self, eng, out, in0, in1, op):
        if eng == "pool":
            eng = "dve"
        return self.add(eng, lambda e: e.tensor_tensor(out, in0, in1, op), [in0, in1], [out])

    def ts(self, eng, out, in0, s1, s2=None, op0=ALU.mult, op1=None, accum_out=None):
        reads = [in0]
        if not isinstance(s1, (int, float)):
            reads.append(s1)
        if s2 is not None and not isinstance(s2, (int, float)):
            reads.append(s2)
        writes = [out]
        kw = {}
        if op1 is not None:
            kw["op1"] = op1
        if accum_out is not None:
            kw["accum_out"] = accum_out
            writes.append(accum_out)
        return self.add(eng, lambda e: e.tensor_scalar(out, in0, s1, s2, op0, **kw), reads, writes)

    def stt(self, eng, out, in0, scalar, in1, op0, op1):
        reads = [in0, in1]
        if not isinstance(scalar, (int, float)):
            reads.append(scalar)
        return self.add(eng, lambda e: e.scalar_tensor_tensor(out, in0, scalar, in1, op0, op1),
                        reads, [out])

    def copy(self, eng, out, in_):
        if eng == "pool":
            self.cp_i = getattr(self, "cp_i", 0) + 1
            eng = "act" if self.cp_i % 2 else "dve"
        if eng == "act":
            return self.add("act", lambda e: e.copy(out, in_), [in_], [out])
        return self.add(eng, lambda e: e.tensor_copy(out, in_), [in_], [out])

    def memset(self, eng, out, val):
        return self.add(eng, lambda e: e.memset(out, val), [], [out])

    def dma(self, q, out, in_, key):
        return self.add(q, lambda e: e.dma_start(out=out, in_=in_), [in_], [out], is_dma=True, key=key)

    def allgather(self, out, in_, groups, key):
        if self.single:
            n = in_.shape[0]
            for r in range(out.shape[0] // n):
                op = self.dma("sp", out[r * n:(r + 1) * n, :], in_, key=key + f"_{r}")
            return op
        return self.add("pool", lambda e: e.collective_compute("AllGather", ALU.bypass, replica_groups=groups,
                                                               ins=[in_], outs=[out]),
                        [in_], [out], is_dma=True, key=key, inc=1)

    def emit(self, final_keys=()):
        nc = self.nc
        from contextlib import ExitStack
        with ExitStack() as st:
            engsem = {}
            for e in ENGS[:4]:
                engsem[e] = st.enter_context(nc.semaphore("s_" + e))
            for k, ent in self.dma_keys.items():
                ent[0] = st.enter_context(nc.semaphore("d_" + k))
            counter = {e: 0 for e in ENGS}
            for op in self.ops:
                if not op.is_dma and op.signal:
                    counter[op.eng] += 1
                    op.count = counter[op.eng]
            known = {e: {} for e in ENGS}
            plan = {e: [] for e in ENGS}
            for op in self.ops:
                waits = []
                kn = known[op.eng]
                for d in op.deps:
                    A = self.ops[d]
                    if A.is_dma:
                        sk = "d_" + A.sem
                        sem = self.dma_keys[A.sem][0]
                        val = A.semval
                    else:
                        sk = A.eng
                        sem = engsem[A.eng]
                        val = A.count
                    if kn.get(sk, 0) >= val:
                        continue
                    kn[sk] = val
                    waits.append((sem, val))
                plan[op.eng].append((op, waits))
            fin = [(self.dma_keys[k][0], self.dma_keys[k][1]) for k in final_keys]
            block = st.enter_context(nc.Block())

            def run(engname):
                def body(e):
                    for op, waits in plan[engname]:
                        for sem, val in waits:
                            e.wait_ge(sem, val)
                        ins = op.emit(e)
                        if op.is_dma:
                            ins.then_inc(self.dma_keys[op.sem][0], op.inc)
                        elif op.signal:
                            ins.then_inc(engsem[engname], 1)
                    if engname == "sp":
                        for sem, val in fin:
                            e.wait_ge(sem, val)
                return body

            block.tensor(run("pe"))
            block.scalar(run("act"))
            block.vector(run("dve"))
            block.gpsimd(run("pool"))
            block.sync(run("sp"))


D = 1024
NB = 2
SEQ = 8192
DEPTH = 2
NCORE = 8
RPB = 4
T = SEQ // RPB
HALO = 16
NH = 4
IN_COLS = 9736
C_DQ, C_DK, C_DV, C_DZ, C_DB, C_DA = 0, 512, 1024, 1536, 2048, 2052
C_PL, C_SU, C_SV = 2056, 2568, 3080
C_RQ, C_RK, C_RV, C_RG = 3592, 4104, 4616, 5128
C_GATES = 5640
EPS = 1e-6
GROUPS = [[0, 1, 2, 3], [4, 5, 6, 7]]
DEBUG = {}


class Ctx:
    pass


def _prod(s):
    n = 1
    for v in s:
        n *= v
    return n


def build(stop_after=None, dumps=(), single=False):
    nc = bass.Bass("TRN2", target_bir_lowering=False)
    P = Prog(nc)
    P.single = single
    K = Ctx()
    K.nc, K.P = nc, P
    K.dumps = {}
    K.dump_req = set(dumps)

    def din(name, shape, dt=F32):
        return P.dram(name, shape, dt, "ExternalInput")

    I = {}
    I["x"] = din("x", [T, D])
    I["xh"] = din("xh", [HALO, D])
    I["cT"] = din("cT", [128, 8])
    I["pos"] = din("pos", [1, T], I32)
    I["meta"] = din("meta", [128, 8])
    I["norm1_g"] = din("norm1_g", [DEPTH, D])
    I["norm2_g"] = din("norm2_g", [DEPTH, D])
    I["ada_w"] = din("ada_w", [DEPTH, D, 6 * D])
    I["ada_b"] = din("ada_b", [DEPTH, 6 * D])
    I["w_in"] = din("w_in", [DEPTH, D, IN_COLS])
    I["dn_conv_w"] = din("dn_conv_w", [DEPTH, 4, 1536])
    I["dn_a_log"] = din("dn_a_log", [DEPTH, 4])
    I["dn_dt_bias"] = din("dn_dt_bias", [DEPTH, 4])
    I["dn_norm_g"] = din("dn_norm_g", [DEPTH, 128])
    I["pool_w"] = din("pool_w", [DEPTH, 4, 128, 128])
    I["pool_scale"] = din("pool_scale", [DEPTH, 512])
    I["sg_ln_g"] = din("sg_ln_g", [DEPTH, 512])
    I["sg_ln_b"] = din("sg_ln_b", [DEPTH, 512])
    I["sg_w"] = din("sg_w", [DEPTH, 4, 128, 128])
    I["sg_b"] = din("sg_b", [DEPTH, 4, 128])
    I["ret_gn_g"] = din("ret_gn_g", [DEPTH, 512])
    I["w_br_dn"] = din("w_br_dn", [DEPTH, 512, D])
    I["w_br_pool"] = din("w_br_pool", [DEPTH, 512, D])
    I["w_br_sg"] = din("w_br_sg", [DEPTH, 512, D])
    I["w_br_ret"] = din("w_br_ret", [DEPTH, 512, D])
    I["w_out"] = din("w_out", [DEPTH, D, D])
    I["mlp_w1"] = din("mlp_w1", [DEPTH, D, 4 * D])
    I["mlp_w2"] = din("mlp_w2", [DEPTH, 4 * D, D])
    I["final_g"] = din("final_g", [1, D])
    K.I = I
    K.out = P.dram("out", [T, D], F32, "ExternalOutput")
    K.dn_o = P.dram("dn_o", [NH, 128, T], F32, "Internal")
    K.dn_b = P.dram("dn_b", [NH, 128, T], F32, "Internal")
    K.ret_o = P.dram("ret_o", [NH, 128, T], F32, "Internal")
    K.ret_q = P.dram("ret_q", [NH, 128, T], F32, "Internal")
    K.cs_dram = P.dram("cs_dram", [2, 128, T], F32, "Internal")
    K.st_src = P.dram("st_src", [12 * 128, 128], F32, "Internal")
    K.st_all = P.dram("st_all", [RPB * 12 * 128, 128], F32, "Internal")
    K.xh_src = P.dram("xh_src", [D, HALO], F32, "Internal")
    K.xh_all = P.dram("xh_all", [RPB * D, HALO], F32, "Internal")

    slab = nc.alloc_sbuf_tensor("slab", [128, 8], F32)
    base0 = nc.lookup_mloc(slab).addr
    K.sb_cur = base0 + 32
    K.sb_lim = base0 + 212000
    K.sb_n = 0

    def sb(name, shape, dt=F32, at=None):
        pbytes = _prod(shape[1:]) * ESZ[dt]
        off = K.sb_cur if at is None else at
        K.sb_n += 1
        h = nc.alloc_sbuf_tensor_at(f"{name}_{K.sb_n}", list(shape), dt, offset=off)
        P.reg(h, "SB", off, pbytes)
        if at is None:
            K.sb_cur += (pbytes + 31) // 32 * 32
            assert K.sb_cur <= K.sb_lim, f"SBUF overflow at {name}: {K.sb_cur - base0}"
        return h

    K.sb = sb
    K.psb = []
    for i in range(8):
        h = nc.alloc_psum_tensor(f"psb{i}", [128, 512], F32)
        P.reg(h, "PS", i * 2048, 2048)
        K.psb.append(h)
    K.ps_big_i = 0
    K.ps_small_i = 0

    def ps_big():
        K.ps_big_i = (K.ps_big_i + 1) % 4
        return K.psb[K.ps_big_i]

    def ps_small():
        K.ps_small_i = (K.ps_small_i + 1) % 16
        b, q = divmod(K.ps_small_i, 4)
        return K.psb[4 + b][:, q * 128:(q + 1) * 128]

    K.ps_big, K.ps_small = ps_big, ps_small
    K.ps_i = 0

    def ps_tile():
        K.ps_i = (K.ps_i + 1) % 8
        return K.psb[K.ps_i]

    K.ps_tile = ps_tile

    def dump(name, ap, shape, dt=F32):
        if name not in K.dump_req:
            return
        d = P.dram("dbg_" + name, list(shape), dt, "ExternalOutput")
        K.dumps[name] = d
        P.dma("sp", d, ap, key="dbg_" + name)

    K.dump = dump

    stages(K, stop_after)

    fk = [k for k in P.dma_keys if k.startswith("out")]
    fk += ["dbg_" + n for n in K.dumps]
    P.emit(final_keys=fk)
    return nc, K


def stages(K, stop_after):
    P, sb, I = K.P, K.sb, K.I
    C = Ctx()
    K.C = C
    di = sb("c_di", [128, 128], I32)
    df = sb("c_df", [128, 128], F32)
    P.add("pool", lambda e: e.iota(di[:], [[1, 128]], base=0, channel_multiplier=-1), [], [di[:]])
    P.copy("dve", df[:], di[:])
    C.ident = sb("c_ident", [128, 128], F32)
    C.U = sb("c_U", [128, 128], F32)
    C.negmT = sb("c_negmT", [128, 128], F32)
    C.posm = sb("c_posm", [128, 128], F32)
    C.ones = sb("c_ones", [128, 128], F32)
    C.identb = sb("c_identb", [128, 128], BF16)
    P.ts("dve", C.ident[:], df[:], 0.0, None, op0=ALU.is_equal)
    P.ts("dve", C.U[:], df[:], 0.0, None, op0=ALU.is_ge)
    P.ts("dve", C.negmT[:], df[:], 0.0, -30000.0, op0=ALU.is_lt, op1=ALU.mult)
    P.ts("dve", C.posm[:], df[:], 0.0, -30000.0, op0=ALU.is_ge, op1=ALU.mult)
    C.SL = sb("c_SL", [128, 128], F32)
    P.ts("dve", C.SL[:], df[:], 0.0, None, op0=ALU.is_lt)
    P.memset("pool", C.ones[:], 1.0)
    P.copy("dve", C.identb[:], C.ident[:])
    pi_ = sb("c_pi", [128, 1], I32)
    pm_i = sb("c_pmi", [128, 1], I32)
    pm = sb("c_pm", [128, 1], F32)
    lo = sb("c_lo", [128, 1], F32)
    hi = sb("c_hi", [128, 1], F32)
    t1 = sb("c_t1", [128, 128], F32)
    bd = {}
    P.add("pool", lambda e: e.iota(pi_[:], [[0, 1]], base=0, channel_multiplier=1), [], [pi_[:]])
    for sz in (16, 32, 64):
        bd[sz] = sb(f"c_bd{sz}", [128, 128], F32)
        P.ts("dve", pm_i[:], pi_[:], sz - 1, None, op0=ALU.bitwise_and)
        P.copy("dve", pm[:], pm_i[:])
        P.ts("dve", lo[:], pm[:], -1.0, None, op0=ALU.mult)
        P.ts("dve", hi[:], pm[:], -1.0, float(sz), op0=ALU.mult, op1=ALU.add)
        P.ts("dve", bd[sz][:], df[:], lo[:, 0:1], None, op0=ALU.is_ge)
        P.ts("dve", t1[:], df[:], hi[:, 0:1], None, op0=ALU.is_lt)
        P.tt("dve", bd[sz][:], bd[sz][:], t1[:], ALU.mult)
    C.bd16 = bd[16]
    C.em = [sb(f"c_em{i}", [128, 128], F32) for i in range(3)]
    P.tt("dve", C.em[0][:], bd[32][:], bd[16][:], ALU.subtract)
    P.tt("dve", C.em[1][:], bd[64][:], bd[32][:], ALU.subtract)
    P.ts("dve", C.em[2][:], bd[64][:], -1.0, 1.0, op0=ALU.mult, op1=ALU.add)
    K.dump("ident", C.ident[:], [128, 128])
    K.dump("em0", C.em[0][:], [128, 128])
    K.dump("negmT", C.negmT[:], [128, 128])

    K.xT = sb("xT", [128, 8, HALO + T], F32)
    K.mark0 = K.sb_cur
    load_x(K)
    K.dump("xT", K.xT[:], [128, 8, HALO + T])
    if stop_after == "load_x":
        return
    preamble(K)
    if stop_after == "preamble":
        return
    for l in range(DEPTH):
        layer(K, l, stop_after)
        if stop_after is not None and stop_after.startswith(f"L{l}"):
            return
    final_norm(K)


def load_x(K):
    P, sb, I, C = K.P, K.sb, K.I, K.C
    xin = [sb(f"xin{i}", [128, D], F32) for i in range(2)]
    for tt in range(T // 128):
        xt = xin[tt % 2]
        P.dma("sp", xt[:], I["x"][tt * 128:(tt + 1) * 128, :], key=f"xin{tt % 2}")
        for fc in range(8):
            ps = K.ps_small()
            P.tr(ps, xt[:, fc * 128:(fc + 1) * 128], C.ident[:])
            P.copy("act" if fc % 2 else "dve", K.xT[:, fc, HALO + tt * 128:HALO + (tt + 1) * 128], ps)
    xh = sb("xh", [HALO, D], F32)
    P.dma("sp", xh[:], I["xh"][:, :], key="xh")
    for fc in range(8):
        ps = K.ps_small()
        P.tr(ps[:, 0:HALO], xh[:, fc * 128:(fc + 1) * 128], C.ident[0:HALO, 0:HALO])
        P.copy("dve", K.xT[:, fc, 0:HALO], ps[:, 0:HALO])
    K.sb_cur = K.mark0


def preamble(K):
    P, sb, I, C = K.P, K.sb, K.I, K.C
    NV = 121
    K.VT = [sb(f"VT{l}", [128, NV], F32) for l in range(DEPTH)]
    K.fgT = sb("fgT", [128, 8], F32)
    K.modT = [sb(f"modT{l}", [128, 48], F32) for l in range(DEPTH)]
    K.AB = [sb(f"AB{l}", [128, 32], F32) for l in range(DEPTH)]
    K.cond = sb("cond", [128, 8], F32)
    mark = K.sb_cur
    rows = sb("vrows", [128, 128], F32)
    for l in range(DEPTH):
        srcs = [(0, 8, I["norm1_g"][l].rearrange("(r p) -> r p", p=128)),
                (8, 8, I["norm2_g"][l].rearrange("(r p) -> r p", p=128)),
                (16, 48, I["ada_b"][l].rearrange("(r p) -> r p", p=128)),
                (64, 48, I["dn_conv_w"][l].rearrange("k (c p) -> (k c) p", p=128)),
                (112, 1, I["dn_norm_g"][l].rearrange("(r p) -> r p", p=128)),
                (113, 4, I["pool_scale"][l].rearrange("(r p) -> r p", p=128)),
                (117, 4, I["ret_gn_g"][l].rearrange("(r p) -> r p", p=128))]
        for (r0, n, src) in srcs:
            P.dma("sp", rows[r0:r0 + n, :], src, key=f"vrows{r0}")
        ps = K.ps_small()
        P.tr(ps[:, 0:NV], rows[0:NV, :], C.ident[0:NV, 0:NV])
        P.copy("dve", K.VT[l][:, :], ps[:, 0:NV])
    P.dma("sp", rows[0:8, :], I["final_g"][0].rearrange("(r p) -> r p", p=128), key="vrows0")
    ps = K.ps_small()
    P.tr(ps[:, 0:8], rows[0:8, :], C.ident[0:8, 0:8])
    P.copy("dve", K.fgT[:, :], ps[:, 0:8])
    ct = sb("ct", [128, 8], F32)
    P.dma("sp", ct[:], I["cT"][:, :], key="ct")
    P.act(K.cond[:], ct[:], AF.Silu)
    PC = 768
    wa = [sb(f"wada{i}", [128, 8, PC], F32) for i in range(2)]
    n = 0
    rowbuf = sb("modrow", [1, 6 * D], F32)
    for l in range(DEPTH):
        wsrc = I["ada_w"][l].rearrange("(kc p) n -> p kc n", p=128)
        for pc in range(6 * D // PC):
            wt = wa[n % 2]
            P.dma("sp", wt[:], wsrc[:, :, pc * PC:(pc + 1) * PC], key=f"wada{n % 2}")
            n += 1
            for hf in range(2):
                psr = K.ps_tile()
                c0_ = hf * (PC // 2)
                for kc in range(8):
                    P.mm(psr[0:1, 0:PC // 2], K.cond[:, kc:kc + 1], wt[:, kc, c0_:c0_ + PC // 2],
                         start=(kc == 0), stop=(kc == 7))
                P.copy("act" if hf else "dve", rowbuf[0:1, pc * PC + c0_:pc * PC + c0_ + PC // 2], psr[0:1, 0:PC // 2])
        psm = K.ps_tile()
        for oc in range(48):
            P.tr(psm[:, oc:oc + 1], rowbuf[0:1, oc * 128:(oc + 1) * 128], C.ident[0:1, 0:1])
        P.tt("dve", K.modT[l][:, :], psm[:, 0:48], K.VT[l][:, 16:64], ALU.add)
        P.stt("dve", K.AB[l][:, 0:8], K.modT[l][:, 8:16], 1.0, K.VT[l][:, 0:8], ALU.add, ALU.mult)
        P.stt("dve", K.AB[l][:, 16:24], K.modT[l][:, 32:40], 1.0, K.VT[l][:, 8:16], ALU.add, ALU.mult)
        K.dump(f"modT{l}", K.modT[l][:, :], [128, 48])
    K.sb_cur = mark
    K.meta = sb("meta", [128, 8], F32)
    P.dma("sp", K.meta[:], I["meta"][:, :], key="meta")
    K.maskj = sb("maskj", [128, 1], F32)
    P.ts("dve", K.maskj[:], K.meta[:, 0:1], 1.0, None, op0=ALU.min)
    K.invc = sb("invc", [128, NH, HALO], F32)
    fi = sb("iv_fi", [128, HALO], I32)
    ff = sb("iv_ff", [128, HALO], F32)
    tb = sb("iv_tb", [128, HALO], F32)
    P.add("pool", lambda e: e.iota(fi[:], [[1, HALO]], base=1, channel_multiplier=0), [], [fi[:]])
    P.copy("dve", ff[:], fi[:])
    for g in range(NH):
        win = float(2 << g)
        P.ts("dve", tb[:], ff[:], win, None, op0=ALU.min)
        P.add("dve", lambda e: e.reciprocal(tb[:], tb[:]), [tb[:]], [tb[:]])
        P.ts("dve", K.invc[:, g, :], tb[:], -1.0, 1.0 / win, op0=ALU.mult, op1=ALU.add)
        P.stt("dve", K.invc[:, g, :], K.invc[:, g, :], K.maskj[:, 0:1], tb[:], ALU.mult, ALU.add)
    K.rstd = sb("rstd", [128, HALO + T], F32)
    rope_ret_consts(K)


LOG_GAMMA = [math.log1p(-2.0 ** (-5.0 - h)) for h in range(NH)]


def rope_ret_consts(K):
    P, sb, I, C = K.P, K.sb, K.I, K.C
    C.Rm = sb("c_Rm", [128, 128], F32)
    C.decT = sb("c_decT", [128, NH, 128], F32)
    C.xi = sb("c_xi", [128, NH, 128], F32)
    C.zeta = sb("c_zeta", [128, NH], F32)
    C.negpi = sb("c_negpi", [128, 1], F32)
    P.memset("pool", C.negpi[:], -math.pi)
    mark = K.sb_cur
    C.cosT = sb("cosT", [128, T], F32)
    C.sinT = sb("sinT", [128, T], F32)
    df = sb("t_df", [128, 128], F32)
    di = sb("t_di", [128, 128], I32)
    t1 = sb("t_t1", [128, 128], F32)
    P.add("pool", lambda e: e.iota(di[:], [[1, 128]], base=0, channel_multiplier=-1), [], [di[:]])
    P.copy("dve", df[:], di[:])
    P.ts("dve", C.Rm[:], df[:], 64.0, None, op0=ALU.is_equal)
    P.ts("dve", t1[:], df[:], -64.0, None, op0=ALU.is_equal)
    P.tt("dve", C.Rm[:], C.Rm[:], t1[:], ALU.subtract)
    fi = sb("t_fi", [128, 128], I32)
    ff = sb("t_ff", [128, 128], F32)
    P.add("pool", lambda e: e.iota(fi[:], [[1, 128]], base=1, channel_multiplier=0), [], [fi[:]])
    P.copy("dve", ff[:], fi[:])
    pi_ = sb("t_pi", [128, 1], I32)
    pf = sb("t_pf", [128, 1], F32)
    P.add("pool", lambda e: e.iota(pi_[:], [[0, 1]], base=127, channel_multiplier=-1), [], [pi_[:]])
    P.copy("dve", pf[:], pi_[:])
    for h in range(NH):
        lg = LOG_GAMMA[h]
        P.act(t1[:], df[:], AF.Exp, scale=lg)
        P.stt("dve", C.decT[:, h, :], t1[:], 128.0 ** -0.5, C.U[:], ALU.mult, ALU.mult)
        P.act(C.xi[:, h, :], ff[:], AF.Exp, scale=lg)
        P.act(C.zeta[:, h:h + 1], pf[:], AF.Exp, scale=lg)
    P.ts("dve", C.zeta[:, :], C.zeta[:, :], 128.0 ** -0.5, None, op0=ALU.mult)
    inv = sb("t_inv", [128, 1], F32)
    meta2 = sb("t_meta", [128, 8], F32)
    P.dma("sp", meta2[:], I["meta"][:, :], key="meta2")
    P.copy("dve", inv[:], meta2[:, 2:3])
    posi = sb("t_posi", [128, T], I32)
    P.dma("sp", posi[:], I["pos"][0:1, :].to_broadcast([128, T]), key="posi")
    ang = sb("t_ang", [128, T], F32)
    P.copy("dve", ang[:], posi[:])
    P.ts("dve", ang[:], ang[:], inv[:, 0:1], None, op0=ALU.mult)
    TWO_PI = 2.0 * math.pi
    ki = sb("t_ki", [128, T], I32)
    kf = sb("t_kf", [128, T], F32)

    def sin_table(dst, shift):
        a = dst
        if shift != 0.0:
            P.ts("dve", a, ang[:], shift, None, op0=ALU.add)
        else:
            P.copy("dve", a, ang[:])
        P.ts("dve", kf[:], a, 1.0 / TWO_PI, None, op0=ALU.mult)
        P.copy("dve", ki[:], kf[:])
        P.copy("dve", kf[:], ki[:])
        P.stt("dve", a, kf[:], -6.28125, a, ALU.mult, ALU.add)
        P.stt("dve", a, kf[:], -(TWO_PI - 6.28125), a, ALU.mult, ALU.add)
        P.ts("dve", kf[:], a, math.pi, -TWO_PI, op0=ALU.is_gt, op1=ALU.mult)
        P.tt("dve", a, a, kf[:], ALU.add)
        P.ts("dve", kf[:], a, -math.pi, TWO_PI, op0=ALU.is_lt, op1=ALU.mult)
        P.tt("dve", a, a, kf[:], ALU.add)
        P.ts("dve", a, a, -math.pi, math.pi, op0=ALU.max, op1=ALU.min)
        P.act(a, a, AF.Sin)

    sin_table(C.sinT[:], 0.0)
    sin_table(C.cosT[:], math.pi / 2.0)
    P.dma("sp", K.cs_dram[0], C.cosT[:], key="cs_w0")
    P.dma("sp", K.cs_dram[1], C.sinT[:], key="cs_w1")
    K.dump("cosT", C.cosT[:], [128, T])
    K.dump("sinT", C.sinT[:], [128, T])
    K.dump("decT", C.decT[:], [128, NH, 128])
    K.sb_cur = mark


def rsqrt(K, out, in_, mul, add):
    P = K.P
    P.act(out, in_, AF.Ln, bias=float(add), scale=float(mul))
    P.act(out, out, AF.Exp, scale=-0.5)


def norm_h(K, A, B, col0, ncol, hT, hcol0, compute_rstd=True):
    P, C = K.P, K.C
    c = 0
    while c < ncol:
        n = min(512, ncol - c)
        xs = lambda fc: K.xT[:, fc, col0 + c:col0 + c + n]
        rs = K.rstd[:, col0 + c:col0 + c + n]
        if compute_rstd:
            ps = K.ps_big()
            for fc in range(8):
                sq = K.sq[fc % 2]
                P.act(sq[:, 0:n], xs(fc), AF.Square)
                P.mm(ps[:, 0:n], C.ones[:, :], sq[:, 0:n], start=(fc == 0), stop=(fc == 7))
            rsqrt(K, rs, ps[:, 0:n], 1.0 / D, EPS)
        for fc in range(8):
            tmp = K.sq[fc % 2]
            P.tt("dve" if fc % 2 else "pool", tmp[:, 0:n], xs(fc), rs, ALU.mult)
            P.act(hT[:, fc, hcol0 + c:hcol0 + c + n], tmp[:, 0:n], AF.Identity, bias=B[:, fc:fc + 1],
                  scale=A[:, fc:fc + 1])
        c += n


def load_w(K, wt, src2d, c0, ncols, key):
    nk = src2d.shape[0] // 128
    src = src2d.rearrange("(kc p) n -> p kc n", p=128)[:, :, c0:c0 + ncols]
    K.P.dma("pool", wt[:, 0:nk, 0:ncols], src, key=key)


class WStream:
    def __init__(self, bufs, loadfns, dist):
        self.bufs, self.fns, self.dist = bufs, loadfns, dist
        self.issued = 0
        for _ in range(min(dist, len(loadfns))):
            self._issue()

    def _issue(self):
        i = self.issued
        self.fns[i](self.bufs[i % len(self.bufs)], i % len(self.bufs))
        self.issued += 1

    def get(self, i):
        while self.issued <= min(i + self.dist, len(self.fns) - 1):
            self._issue()
        return self.bufs[i % len(self.bufs)]


def layer_setup(K, l):
    P, sb, I, C = K.P, K.sb, K.I, K.C
    L = K.L
    L.dtb = sb("dtb", [128, 4], F32)
    L.nexpA = sb("nexpA", [128, 4], F32)
    P.dma("sp", L.dtb[:], I["dn_dt_bias"][l:l + 1, :].to_broadcast([128, 4]), key="dtb")
    P.dma("sp", L.nexpA[:], I["dn_a_log"][l:l + 1, :].to_broadcast([128, 4]), key="alog")
    P.act(L.nexpA[:], L.nexpA[:], AF.Exp)
    P.ts("dve", L.nexpA[:], L.nexpA[:], -1.0, None, op0=ALU.mult)
    NT = T // 128
    L.beta = sb("beta", [128, NT, 4], F32)
    L.nbeta = sb("nbeta", [128, NT, 4], F32)
    L.g = sb("g", [128, NT, 4], F32)
    L.gc = sb("gc", [128, NT, 4], F32)
    L.ngc = sb("ngc", [128, NT, 4], F32)
    L.bg = sb("bg", [128, NT, 4], F32)
    L.S = [sb(f"S{h}", [128, 128], F32) for h in range(NH)]
    L.Pref = [[sb(f"Pref{h}_{i}", [128, 128], F32) for i in range(2)] for h in range(NH)]
    L.pref_i = [0] * NH
    L.carry = sb("carry", [128, NH, 3, HALO], F32)
    for h in range(NH):
        P.memset("pool", L.S[h][:], 0.0)
        P.copy("pool", L.Pref[h][0][:], C.ident[:])
    L.Sr = [sb(f"Sr{h}", [128, 128], F32) for h in range(NH)]
    for h in range(NH):
        P.memset("pool", L.Sr[h][:], 0.0)
    L.wbg = sb("wbg", [128, 8, 8], BF16)
    load_w(K, L.wbg, I["w_in"][l], C_DB, 8, key="wbg")


def layer_setup_c(K, l):
    P, sb, I, C = K.P, K.sb, K.I, K.C
    L = K.L
    L.Sin = sb("Sin", [128, NH, 128], F32)
    L.Srin = sb("Srin", [128, NH, 128], F32)
    L.pcarry = sb("pcarry", [128, NH, HALO], F32)
    L.lng = sb("lng", [128, 512], F32)
    L.lnb = sb("lnb", [128, 512], F32)
    P.dma("sp", L.lng[:], I["sg_ln_g"][l:l + 1, :].to_broadcast([128, 512]), key="lng")
    P.dma("sp", L.lnb[:], I["sg_ln_b"][l:l + 1, :].to_broadcast([128, 512]), key="lnb")
    L.sgb = sb("sgb", [128, NH * 128], F32)
    P.dma("sp", L.sgb[:], I["sg_b"][l:l + 1].rearrange("o g n -> o (g n)").to_broadcast([128, NH * 128]), key="sgb")
    L.WcT = sb("WcT", [128, NH, 128], BF16)
    m_ = K.sb_cur
    sgw = sb("sgw", [128, NH, 128], F32)
    P.dma("sp", sgw[:], I["sg_w"][l].rearrange("g t s -> t g s"), key="sgw")
    psw = K.ps_tile()
    for g in range(NH):
        P.tr(psw[:, g * 128:(g + 1) * 128], sgw[:, g, :], C.ident[:])
    for g in range(NH):
        P.tt("dve", L.WcT[:, g, :], psw[:, g * 128:(g + 1) * 128], C.U[:], ALU.mult)
    K.sb_cur = m_


def bg_unit(K, hT, hh):
    P, C, L = K.P, K.C, K.L
    ps = K.ps_tile()
    for tt in range(8):
        for kc in range(8):
            P.mm(ps[:, tt * 8:tt * 8 + 8], hT[:, kc, HALO + tt * 128:HALO + (tt + 1) * 128], L.wbg[:, kc, 0:8],
                 start=(kc == 0), stop=(kc == 7))
    pv = ps[:, 0:64].rearrange("p (t c) -> p t c", c=8)
    sl = slice(hh * 8, hh * 8 + 8)
    P.act(L.beta[:, sl, :], pv[:, :, 0:4], AF.Sigmoid)
    P.ts("dve", L.nbeta[:, sl, :], L.beta[:, sl, :], -1.0, None, op0=ALU.mult)
    tmp = L.bg[:, sl, :]
    for tt in range(8):
        P.tt("dve", L.g[:, hh * 8 + tt, :], pv[:, tt, 4:8], L.dtb[:, :], ALU.add)
    P.act(L.g[:, sl, :], L.g[:, sl, :], AF.Exp)
    P.act(L.g[:, sl, :], L.g[:, sl, :], AF.Ln, bias=1.0)
    for tt in range(8):
        P.tt("dve", L.g[:, hh * 8 + tt, :], L.g[:, hh * 8 + tt, :], L.nexpA[:, :], ALU.mult)
    ps2 = K.ps_tile()
    for tt in range(8):
        P.mm(ps2[:, tt * 4:tt * 4 + 4], C.U[:, :], L.g[:, hh * 8 + tt, :])
    pv2 = ps2[:, 0:32].rearrange("p (t c) -> p t c", c=4)
    P.copy("dve", L.gc[:, sl, :], pv2)
    P.ts("dve", L.ngc[:, sl, :], pv2, -1.0, None, op0=ALU.mult)
    P.act(tmp, pv2, AF.Exp)
    P.tt("dve", L.bg[:, sl, :], tmp, L.beta[:, sl, :], ALU.mult)


def dn_head_unit(K, hT, hh, h):
    P, sb, I, C, L = K.P, K.sb, K.I, K.C, K.L
    l = L.l
    HT = 1024
    mark = K.sb_cur
    alias_base = K.sb_cur
    w3 = sb("w3", [128, 3, 8, 128], BF16)
    for ch, c0 in enumerate((C_DQ, C_DK, C_DV)):
        load_w(K, w3[:, ch], I["w_in"][l], c0 + h * 128, 128, key=f"w3_{ch}")
    pre = sb("pre", [128, 3, HALO + HT], F32)
    alias_end = K.sb_cur
    post = sb("post", [128, 3, HT], F32)
    ostage = sb("ostage", [128, HT], F32)
    bstage = sb("bstage", [128, HT], F32)
    for ch in range(3):
        if hh == 0:
            ps = K.ps_tile()
            for kc in range(8):
                P.mm(ps[:, 0:HALO], w3[:, ch, kc, :], hT[:, kc, 0:HALO], start=(kc == 0), stop=(kc == 7))
            P.ts("dve", pre[:, ch, 0:HALO], ps[:, 0:HALO], K.maskj[:, 0:1], None, op0=ALU.mult)
        else:
            P.copy("pool", pre[:, ch, 0:HALO], L.carry[:, h, ch, :])
        for gi in range(2):
            ps = K.ps_tile()
            for kc in range(8):
                P.mm(ps[:, :], w3[:, ch, kc, :], hT[:, kc, HALO + gi * 512:HALO + (gi + 1) * 512],
                     start=(kc == 0), stop=(kc == 7))
            P.copy("act", pre[:, ch, HALO + gi * 512:HALO + (gi + 1) * 512], ps[:, :])
        if hh == 0:
            P.copy("pool", L.carry[:, h, ch, :], pre[:, ch, HT:HT + HALO])
    for ch in range(3):
        cc = ch * 4 + h
        wcol = lambda k: K.VT[l][:, 64 + k * 12 + cc:64 + k * 12 + cc + 1]
        dst = post[:, ch, :]
        P.act(dst, pre[:, ch, HALO - 3:HALO - 3 + HT], AF.Copy, scale=wcol(0))
        for k in range(1, 4):
            P.stt("dve", dst, pre[:, ch, HALO - 3 + k:HALO - 3 + k + HT], wcol(k), dst, ALU.mult, ALU.add)
        P.act(dst, dst, AF.Silu)
    for ch in range(2):
        for gi in range(2):
            seg = post[:, ch, gi * 512:(gi + 1) * 512]
            sq = K.sq[gi % 2]
            P.act(sq[:, :], seg, AF.Square)
            ps = K.ps_tile()
            P.mm(ps[:, :], C.ones[:, :], sq[:, :])
            rsqrt(K, sq[:, :], ps[:, :], 1.0, EPS)
            if ch == 0:
                P.stt("dve", seg, seg, 128.0 ** -0.5, sq[:, :], ALU.mult, ALU.mult)
            else:
                P.tt("pool", seg, seg, sq[:, :], ALU.mult)
    K.dump(f"dnq{l}_{hh}_{h}", post[:, :, :], [128, 3, HT])
    names = ["gB", "gU", "gSL", "EGb", "DT", "Dm", "Nm", "qkDT", "qsT", "ks", "kbg", "vb", "u", "wT", "w", "vnew", "TT", "AT"]
    W = [{n: sb(f"dn_{n}{i}", [128, 128], F32) for n in names} for i in range(2)]
    MN = [[sb(f"dn_MN{i}_{k}", [128, 256], F32) for k in range(2)] for i in range(2)]
    for i in range(2):
        W[i]["Ne"] = sb(f"dn_Ne{i}", [128, 128], F32)
        W[i]["XT1"] = sb(f"dn_XT1{i}", [128, 256], F32)
    acur = [alias_base]

    def sba(name, shape):
        t = sb(name, shape, F32, at=acur[0])
        acur[0] += _prod(shape[1:]) * 4
        assert acur[0] <= alias_end
        return t

    W.append({n: sba(f"dn_{n}2", [128, 128]) for n in names})
    W[2]["Ne"] = sba("dn_Ne2", [128, 128])
    W[2]["XT1"] = sba("dn_XT12", [128, 256])
    MN.append([sba(f"dn_MN2_{k}", [128, 256]) for k in range(2)])
    Y = [[sb(f"dn_Y{i}_{k}", [128, 128], F32) for k in range(2)] for i in range(2)]
    Y.append([sba(f"dn_Y2_{k}", [128, 128]) for k in range(2)])
    S = L.S[h]

    def chunk_gen(c):
        cg = hh * 8 + c
        w = W[c % 3]
        t0 = c * 128
        qT = post[:, 0, t0:t0 + 128]
        kT = post[:, 1, t0:t0 + 128]
        vT = post[:, 2, t0:t0 + 128]
        col = lambda t: t[:, cg, h:h + 1]
        psA = K.ps_tile()
        P.tr(psA[:, 0:128], kT, C.ident[:])
        P.tr(psA[:, 128:256], vT, C.ident[:])
        yield
        P.copy("pool", w["gB"][:], L.g[:, cg, h:h + 1].to_broadcast([128, 128]))
        P.ts("dve", w["gU"][:], C.U[:], col(L.g), None, op0=ALU.mult)
        P.ts("dve", w["gSL"][:], C.SL[:], col(L.g), None, op0=ALU.mult)
        psB = K.ps_tile()
        P.mm(psB[:, 0:128], w["gB"][:], C.U[:])
        P.mm(psB[:, 128:256], w["gSL"][:], C.U[:], start=True, stop=False)
        P.mm(psB[:, 128:256], C.ident[:], C.negmT[:], start=False, stop=True)
        P.mm(psB[:, 256:384], w["gU"][:], C.SL[:], start=True, stop=False)
        P.mm(psB[:, 256:384], C.ident[:], C.posm[:], start=False, stop=True)
        P.act(w["EGb"][:], psB[:, 0:128], AF.Exp)
        P.act(w["DT"][:], psB[:, 128:256], AF.Exp)
        P.act(w["Dm"][:], psB[:, 256:384], AF.Exp)
        egl = w["EGb"][:, 127:128]
        yield
        psC = K.ps_tile()
        P.mm(psC[:, 0:128], kT, kT)
        P.mm(psC[:, 128:256], kT, qT)
        P.stt("dve", w["Nm"][:], psC[:, 0:128], col(L.nbeta), w["Dm"][:], ALU.mult, ALU.mult)
        P.tt("dve", w["qkDT"][:], psC[:, 128:256], w["DT"][:], ALU.mult)
        P.tt("pool", w["qsT"][:], qT, w["EGb"][:], ALU.mult)
        P.act(w["ks"][:], psA[:, 0:128], AF.Copy, scale=w["DT"][:, 127:128])
        P.act(w["kbg"][:], psA[:, 0:128], AF.Copy, scale=col(L.bg))
        P.act(w["vb"][:], psA[:, 128:256], AF.Copy, scale=col(L.beta))
        yield
        mn = MN[c % 3]
        yy = Y[c % 3]
        P.tt("pool", mn[0][:, 128:256], w["Nm"][:], C.bd16[:], ALU.mult)
        psD = K.ps_tile()
        P.tr(psD[:, 0:128], mn[0][:, 128:256], C.ident[:])
        P.copy("act", mn[0][:, 0:128], psD[:, 0:128])
        P.tt("dve", yy[0][:], psD[:, 0:128], C.ident[:], ALU.add)
        yield
        cur = 0
        ycur = 0
        for lev in range(3):
            Ma, Na = mn[cur][:, 0:128], mn[cur][:, 128:256]
            nxt = 1 - cur
            psE = K.ps_tile()
            if lev < 2:
                P.mm(psE[:, 0:128], Na, Ma)
                P.mm(psE[:, 128:256], Ma, Na)
                P.copy("act", mn[nxt][:, 0:256], psE[:, 0:256])
            else:
                P.mm(psE[:, 128:256], Ma, Na)
                P.copy("act", mn[nxt][:, 128:256], psE[:, 128:256])
            yield
            psF = K.ps_tile()
            P.mm(psF[:, 0:128], mn[nxt][:, 128:256], yy[ycur][:])
            P.tt("dve", yy[1 - ycur][:], yy[ycur][:], psF[:, 0:128], ALU.add)
            ycur = 1 - ycur
            cur = nxt
        for lev in range(3):
            yield
            P.tt("pool", w["Ne"][:], w["Nm"][:], C.em[lev][:], ALU.mult)
            psE = K.ps_tile()
            P.tr(psE[:, 0:128], yy[ycur][:], C.ident[:])
            P.mm(psE[:, 128:256], w["Ne"][:], yy[ycur][:])
            P.copy("act", w["XT1"][:, 0:256], psE[:, 0:256])
            yield
            psF = K.ps_tile()
            P.mm(psF[:, 0:128], w["XT1"][:, 0:128], w["XT1"][:, 128:256])
            P.tt("dve", yy[1 - ycur][:], yy[ycur][:], psF[:, 0:128], ALU.add)
            ycur = 1 - ycur
        XT = yy[ycur]
        yield
        psG = K.ps_tile()
        P.mm(psG[:, 0:128], XT[:], w["vb"][:])
        P.mm(psG[:, 128:256], w["kbg"][:], XT[:])
        P.mm(psG[:, 256:384], XT[:], w["kbg"][:])
        P.copy("act", w["u"][:], psG[:, 0:128])
        P.copy("act", w["wT"][:], psG[:, 128:256])
        P.copy("act", w["w"][:], psG[:, 256:384])
        yield
        psH = K.ps_tile()
        P.mm(psH[:, 0:128], w["wT"][:], S[:])
        P.tt("dve", w["vnew"][:], w["u"][:], psH[:, 0:128], ALU.subtract)
        psI = K.ps_tile()
        P.mm(psI[:, 0:128], S[:], w["qsT"][:], start=True, stop=False)
        P.mm(psI[:, 0:128], w["vnew"][:], w["qkDT"][:], start=False, stop=True)
        P.copy("act", ostage[:, t0:t0 + 128], psI[:, 0:128])
        psJ = K.ps_tile()
        P.mm(psJ[:, 0:128], w["ks"][:], w["vnew"][:])
        P.stt("dve", S[:], S[:], egl, psJ[:, 0:128], ALU.mult, ALU.add)
        pr = L.Pref[h][L.pref_i[h]]
        prn = L.Pref[h][1 - L.pref_i[h]]
        L.pref_i[h] = 1 - L.pref_i[h]
        psK = K.ps_tile()
        P.mm(psK[:, 0:128], w["w"][:], w["ks"][:])
        P.mm(psK[:, 128:256], w["w"][:], w["qkDT"][:])
        P.stt("dve", w["TT"][:], C.ident[:], egl, psK[:, 0:128], ALU.mult, ALU.subtract)
        P.tt("dve", w["AT"][:], w["qsT"][:], psK[:, 128:256], ALU.subtract)
        psL = K.ps_tile()
        P.mm(psL[:, 0:128], pr[:], w["AT"][:])
        P.mm(psL[:, 128:256], w["TT"][:], pr[:])
        P.copy("act", bstage[:, t0:t0 + 128], psL[:, 0:128])
        P.copy("act", prn[:], psL[:, 128:256])

    for a in range(0, 8, 3):
        active = [chunk_gen(c_) for c_ in range(a, min(a + 3, 8))]
        while active:
            for g_ in list(active):
                try:
                    next(g_)
                except StopIteration:
                    active.remove(g_)
    P.dma("sp", K.dn_o[h, :, hh * HT:(hh + 1) * HT], ostage[:, :], key=f"dn_o")
    P.dma("sp", K.dn_b[h, :, hh * HT:(hh + 1) * HT], bstage[:, :], key=f"dn_b")
    K.dump(f"dno{l}_{hh}_{h}", ostage[:, :], [128, HT])
    K.sb_cur = mark


def ret_head_unit(K, hT, hh, h, w3=None):
    P, sb, I, C, L = K.P, K.sb, K.I, K.C, K.L
    l = L.l
    HT = 1024
    mark = K.sb_cur
    if w3 is None:
        w3 = sb("rw3", [128, 3, 8, 128], BF16)
        for ch, c0 in enumerate((C_RQ, C_RK, C_RV)):
            load_w(K, w3[:, ch], I["w_in"][l], c0 + h * 128, 128, key=f"rw3_{ch}")
    pre = sb("rpre", [128, 2, HT], F32)
    rot = sb("rrot", [128, 2, HT], F32)
    ostage = sb("rostage", [128, HT], F32)
    qstage = sb("rqstage", [128, HT], F32)
    tcol0 = hh * HT
    for ch in range(2):
        for gi in range(2):
            ps = K.ps_tile()
            for kc in range(8):
                P.mm(ps[:, :], w3[:, ch, kc, :], hT[:, kc, HALO + gi * 512:HALO + (gi + 1) * 512],
                     start=(kc == 0), stop=(kc == 7))
            seg = pre[:, ch, gi * 512:(gi + 1) * 512]
            P.copy("act", seg, ps[:, :])
            ps2 = K.ps_tile()
            P.mm(ps2[:, :], C.Rm[:, :], seg)
            cs = K.cs_half[0][:, gi * 512:(gi + 1) * 512]
            sn = K.cs_half[1][:, gi * 512:(gi + 1) * 512]
            rseg = rot[:, ch, gi * 512:(gi + 1) * 512]
            P.tt("dve", rseg, ps2[:, :], sn, ALU.mult)
            P.tt("pool", seg, seg, cs, ALU.mult)
            P.tt("pool", seg, seg, rseg, ALU.add)
    K.dump(f"rqk{l}_{hh}_{h}", pre[:, :, :], [128, 2, HT])
    if DEBUG.get("ret_cut") == 0:
        K.dump(f"reto{l}_{hh}_{h}", pre[:, 0, :], [128, HT])
        K.sb_cur = mark
        return
    names = ["kz", "v", "scDT", "qx"]
    W = [{n: sb(f"rt_{n}{i}", [128, 128], F32) for n in names} for i in range(4)]
    Sr = L.Sr[h]
    g128 = math.exp(LOG_GAMMA[h] * 128.0)

    def chunk_gen(c):
        cg = hh * 8 + c
        w = W[c % 4]
        t0 = c * 128
        qT = pre[:, 0, t0:t0 + 128]
        kT = pre[:, 1, t0:t0 + 128]
        psA = K.ps_tile()
        P.tr(psA[:, 0:128], kT, C.ident[:])
        for kc in range(8):
            P.mm(psA[:, 128:256], hT[:, kc, HALO + t0:HALO + t0 + 128], w3[:, 2, kc, :],
                 start=(kc == 0), stop=(kc == 7))
        P.mm(psA[:, 256:384], kT, qT)
        yield
        P.act(w["kz"][:], psA[:, 0:128], AF.Copy, scale=C.zeta[:, h:h + 1])
        P.copy("act", w["v"][:], psA[:, 128:256])
        P.tt("dve", w["scDT"][:], psA[:, 256:384], C.decT[:, h, :], ALU.mult)
        P.tt("pool", w["qx"][:], qT, C.xi[:, h, :], ALU.mult)
        yield
        if DEBUG.get("ret_cut") == 1:
            P.copy("act", ostage[:, t0:t0 + 128], w["scDT"][:])
            return
        psB = K.ps_tile()
        P.mm(psB[:, 0:128], w["v"][:], w["scDT"][:], start=True, stop=False)
        P.mm(psB[:, 0:128], Sr[:], w["qx"][:], start=False, stop=True)
        P.copy("act", ostage[:, t0:t0 + 128], psB[:, 0:128])
        psC = K.ps_tile()
        P.mm(psC[:, 0:128], w["kz"][:], w["v"][:])
        P.stt("dve", Sr[:], Sr[:], g128, psC[:, 0:128], ALU.mult, ALU.add)
        P.ts("dve", qstage[:, t0:t0 + 128], w["qx"][:], math.exp(LOG_GAMMA[h] * 128.0 * cg), None, op0=ALU.mult)

    for a in range(0, 8, 4):
        active = [chunk_gen(c_) for c_ in range(a, a + 4)]
        while active:
            for g_ in list(active):
                try:
                    next(g_)
                except StopIteration:
                    active.remove(g_)
    P.dma("sp", K.ret_o[h, :, hh * HT:(hh + 1) * HT], ostage[:, :], key="ret_o")
    P.dma("sp", K.ret_q[h, :, hh * HT:(hh + 1) * HT], qstage[:, :], key="ret_q")
    K.dump(f"reto{l}_{hh}_{h}", ostage[:, :], [128, HT])
    K.sb_cur = mark


def exchange_states(K):
    P, sb, I, C, L = K.P, K.sb, K.I, K.C, K.L
    l = L.l
    mark = K.sb_cur
    stout = sb("stout", [128, 12, 128], F32)
    ps = K.ps_tile()
    for h in range(NH):
        P.tr(ps[:, h * 128:(h + 1) * 128], L.Pref[h][L.pref_i[h]][:], C.ident[:])
    P.copy("act", stout[:, 0:4, :], ps[:, :].rearrange("p (h n) -> p h n", n=128))
    for h in range(NH):
        P.copy("pool", stout[:, 4 + h, :], L.S[h][:])
        P.copy("pool", stout[:, 8 + h, :], L.Sr[h][:])
    P.dma("sp", K.st_src.rearrange("(i p) n -> p i n", p=128), stout[:, :, :], key="st_src")
    P.allgather(K.st_all, K.st_src, GROUPS, key=f"ag_st{l}")
    K.sb_cur = mark


def exchange_finish(K):
    P, sb, I, C, L = K.P, K.sb, K.I, K.C, K.L
    l = L.l
    mark = K.sb_cur
    stin = sb("stin", [128, 3, 12, 128], F32)
    src = K.st_all.rearrange("(r i p) n -> p r i n", r=RPB, i=12, p=128)
    for r in range(3):
        P.dma("sp", stin[:, r], src[:, r], key=f"stin{r}")
    mr = sb("mr", [128, 4], F32)
    aa = sb("aa", [128, 4], F32)
    cf = sb("cf", [128, 16], F32)
    for r in range(3):
        P.ts("dve", mr[:, r:r + 1], K.meta[:, 0:1], float(r), None, op0=ALU.is_gt)
        P.ts("dve", aa[:, r:r + 1], K.meta[:, 0:1], -(1.0 + r), 0.0, op0=ALU.add, op1=ALU.max)
        for h in range(NH):
            P.act(cf[:, r * 4 + h:r * 4 + h + 1], aa[:, r:r + 1], AF.Exp, scale=LOG_GAMMA[h] * float(T))
        P.ts("dve", cf[:, r * 4:r * 4 + 4], cf[:, r * 4:r * 4 + 4], mr[:, r:r + 1], None, op0=ALU.mult)
    P.memset("pool", L.Sin[:], 0.0)
    tt_ = [sb(f"ex_t{i}", [128, 128], F32) for i in range(2)]
    for h in range(NH):
        for r in range(3):
            t = tt_[(h * 3 + r) % 2]
            ps = K.ps_tile()
            P.mm(ps[:, 0:128], stin[:, r, h, :], L.Sin[:, h, :])
            P.tt("dve", t[:], ps[:, 0:128], stin[:, r, 4 + h, :], ALU.add)
            P.tt("pool", t[:], t[:], L.Sin[:, h, :], ALU.subtract)
            P.stt("dve", L.Sin[:, h, :], t[:], mr[:, r:r + 1], L.Sin[:, h, :], ALU.mult, ALU.add)
        P.ts("dve", L.Srin[:, h, :], stin[:, 0, 8 + h, :], cf[:, h:h + 1], None, op0=ALU.mult)
        for r in (1, 2):
            P.stt("dve", L.Srin[:, h, :], stin[:, r, 8 + h, :], cf[:, r * 4 + h:r * 4 + h + 1], L.Srin[:, h, :],
                  ALU.mult, ALU.add)
    K.dump(f"Sin{l}", L.Sin[:], [128, NH, 128])
    K.dump(f"Srin{l}", L.Srin[:], [128, NH, 128])
    K.sb_cur = mark


def inproj_fm(K, hT, wt, n0, n, dst_fn):
    P = K.P
    c = 0
    while c < n:
        m = min(512, n - c)
        ps = K.ps_tile()
        for kc in range(8):
            P.mm(ps[:, 0:m], wt[:, kc, :], hT[:, kc, n0 + c:n0 + c + m], start=(kc == 0), stop=(kc == 7))
        dst_fn(ps, c, m)
        c += m


def phase_c_branches(K, hT, hh, yT):
    P, sb, I, C, L = K.P, K.sb, K.I, K.C, K.L
    l = L.l
    HT = 1024
    c0 = hh * HT

    def run_rr(gens):
        active = list(gens)
        while active:
            for g_ in list(active):
                try:
                    next(g_)
                except StopIteration:
                    active.remove(g_)

    mark = K.sb_cur
    pre = sb("b_pre", [128, HALO + HT], F32)
    sA = sb("b_sA", [128, HALO + HT], F32)
    sB = sb("b_sB", [128, HALO + HT], F32)
    pooled = sb("b_pooled", [128, HT], BF16)
    wp = [sb(f"b_wp{i}", [128, 8, 128], BF16) for i in range(2)]
    wpool = sb("b_wpool", [128, NH, 128], BF16)
    P.dma("pool", wpool[:, :, :], I["pool_w"][l].rearrange("g c d -> c g d"), key="b_wpool")
    N = HALO + HT
    wsv = sb("s_wsv", [128, 8, 512], BF16)
    load_w(K, wsv, I["w_in"][l], C_SV, 512, key="s_wsv")
    wsu = [sb(f"s_wsu{i}", [128, 8, 128], BF16) for i in range(2)]
    uT = sb("s_uT", [128, NH, HT], BF16)
    gv = [sb(f"s_gv{i}", [128, 512], F32) for i in range(2)]
    junk = sb("s_junk", [128, 512], F32)
    vt = [sb(f"s_vt{i}", [128, 512], BF16) for i in range(2)]
    st = [sb(f"s_st{i}", [128, 8], F32) for i in range(2)]
    wsB = WStream(wp, [(lambda buf, bi, g_=g_: load_w(K, buf, I["w_in"][l], C_PL + g_ * 128, 128, key=f"b_wp{bi}"))
                       for g_ in range(NH)], 1)
    wsC = WStream(wsu, [(lambda buf, bi, g_=g_: load_w(K, buf, I["w_in"][l], C_SU + g_ * 128, 128, key=f"s_wsu{bi}"))
                        for g_ in range(NH)], 1)

    def gen_B():
        for g in range(NH):
            win = 2 << g
            wsB.get(g)
            if hh == 0:
                ps = K.ps_tile()
                for kc in range(8):
                    P.mm(ps[:, 0:HALO], wp[g % 2][:, kc, :], hT[:, kc, 0:HALO], start=(kc == 0), stop=(kc == 7))
                P.ts("dve", pre[:, 0:HALO], ps[:, 0:HALO], K.maskj[:, 0:1], None, op0=ALU.mult)
            else:
                P.copy("pool", pre[:, 0:HALO], L.pcarry[:, g, :])
            inproj_fm(K, hT, wp[g % 2], HALO, HT, lambda ps, c, m: P.copy("act", pre[:, HALO + c:HALO + c + m], ps[:, 0:m]))
            if hh == 0:
                P.copy("pool", L.pcarry[:, g, :], pre[:, HT:HT + HALO])
            yield
            src, sh, bufs, k = pre, 1, [sA, sB], 0
            while sh < win:
                dst = bufs[k % 2]
                lo = 2 * sh - 1
                P.tt("pool" if k % 2 else "dve", dst[:, lo:N], src[:, lo:N], src[:, lo - sh:N - sh], ALU.add)
                src, sh, k = dst, sh * 2, k + 1
            tmpb = bufs[k % 2]
            P.stt("dve", tmpb[:, HALO:N], src[:, HALO:N], 1.0 / win, pre[:, HALO:N], ALU.mult, ALU.subtract)
            if hh == 0:
                P.tt("dve", tmpb[:, HALO:2 * HALO], src[:, HALO:2 * HALO], K.invc[:, g, :], ALU.mult)
                P.tt("dve", tmpb[:, HALO:2 * HALO], tmpb[:, HALO:2 * HALO], pre[:, HALO:2 * HALO], ALU.subtract)
            yield
            P.copy("pool", pooled[:, :], tmpb[:, HALO:N])
            for gi in range(2):
                cs = slice(gi * 512, (gi + 1) * 512)
                ps = K.ps_tile()
                P.mm(ps[:, :], wpool[:, g, :], pooled[:, cs])
                P.act(yT[:, 4 + g, cs], ps[:, :], AF.Copy, scale=K.VT[l][:, 113 + g:114 + g])
            yield
    def gen_C():
        for g in range(NH):
            wsC.get(g)
            inproj_fm(K, hT, wsu[g % 2], HALO, HT,
                      lambda ps, c, m, g=g: P.act(uT[:, g, c:c + m], ps[:, 0:m], AF.Gelu))
            yield
        for tt in range(8):
            g_ = gv[tt % 2]
            s_ = st[tt % 2]
            ps = K.ps_tile()
            for kc in range(8):
                P.mm(ps[:, :], hT[:, kc, HALO + tt * 128:HALO + (tt + 1) * 128], wsv[:, kc, :],
                     start=(kc == 0), stop=(kc == 7))
            P.act(g_[:, :], ps[:, :], AF.Gelu, accum_out=s_[:, 0:1])
            P.act(junk[:, :], g_[:, :], AF.Square, accum_out=s_[:, 1:2])
            yield
            P.ts("dve", s_[:, 2:3], s_[:, 0:1], 1.0 / 512.0, None, op0=ALU.mult)
            P.tt("dve", s_[:, 3:4], s_[:, 2:3], s_[:, 2:3], ALU.mult)
            P.stt("dve", s_[:, 4:5], s_[:, 1:2], 1.0 / 512.0, s_[:, 3:4], ALU.mult, ALU.subtract)
            rsqrt(K, s_[:, 5:6], s_[:, 4:5], 1.0, EPS)
            P.ts("dve", g_[:, :], g_[:, :], s_[:, 2:3], s_[:, 5:6], op0=ALU.subtract, op1=ALU.mult)
            P.tt("pool", g_[:, :], g_[:, :], L.lng[:, :], ALU.mult)
            P.tt("pool", vt[tt % 2][:, :], g_[:, :], L.lnb[:, :], ALU.add)
            yield
            ps2 = K.ps_tile()
            for g in range(NH):
                P.mm(ps2[:, g * 128:(g + 1) * 128], vt[tt % 2][:, g * 128:(g + 1) * 128], L.WcT[:, g, :])
            P.tt("dve", junk[:, :], ps2[:, :], L.sgb[:, :], ALU.add)
            P.tt("pool", yT[:, 8:12, tt * 128:(tt + 1) * 128], junk[:, :].rearrange("p (g n) -> p g n", n=128),
                 uT[:, :, tt * 128:(tt + 1) * 128], ALU.mult)


    run_rr([gen_B(), gen_C()])
    K.sb_cur = mark
    if hh == 0:
        exchange_finish(K)
    mark = K.sb_cur
    o_sb = sb("a_o", [128, HT], F32)
    b_sb = sb("a_b", [128, HT], F32)
    zs = [sb(f"a_zs{i}", [128, 512], F32) for i in range(2)]
    wz = [sb(f"a_wz{i}", [128, 8, 128], BF16) for i in range(2)]
    sqA = [sb(f"a_sq{i}", [128, 512], F32) for i in range(2)]
    o_sbD = sb("d_o", [128, HT], F32)
    q_sb = sb("d_q", [128, HT], F32)
    zsD = [sb(f"d_zsD{i}", [128, 512], F32) for i in range(2)]
    mt = [sb(f"d_m{i}", [128, 512], F32) for i in range(2)]
    wzD = [sb(f"d_wzD{i}", [128, 8, 128], BF16) for i in range(2)]
    sqD = [sb(f"d_sq{i}", [128, 512], F32) for i in range(2)]
    wsA = WStream(wz, [(lambda buf, bi, h_=h_: load_w(K, buf, I["w_in"][l], C_DZ + h_ * 128, 128, key=f"a_wz{bi}"))
                       for h_ in range(NH)], 1)
    wsD = WStream(wzD, [(lambda buf, bi, h_=h_: load_w(K, buf, I["w_in"][l], C_RG + h_ * 128, 128, key=f"d_wzD{bi}"))
                        for h_ in range(NH)], 1)

    def gen_A():
        for h in range(NH):
            wsA.get(h)
            P.dma("sp", o_sb[:, :], K.dn_o[h, :, c0:c0 + HT], key="a_o")
            P.dma("sp", b_sb[:, :], K.dn_b[h, :, c0:c0 + HT], key="a_b")
            for gi in range(2):
                cs = slice(gi * 512, (gi + 1) * 512)
                ps = K.ps_tile()
                P.mm(ps[:, :], L.Sin[:, h, :], b_sb[:, cs])
                P.tt("dve", o_sb[:, cs], o_sb[:, cs], ps[:, :], ALU.add)
                yield
                sq = sqA[gi % 2]
                P.act(sq[:, :], o_sb[:, cs], AF.Square)
                ps2 = K.ps_tile()
                P.mm(ps2[:, :], C.ones[:, :], sq[:, :])
                rsqrt(K, sq[:, :], ps2[:, :], 1.0 / 128.0, EPS)
                P.stt("dve", o_sb[:, cs], o_sb[:, cs], K.VT[l][:, 112:113], sq[:, :], ALU.mult, ALU.mult)
                yield
                z = zs[gi % 2]
                ps3 = K.ps_tile()
                for kc in range(8):
                    P.mm(ps3[:, :], wz[h % 2][:, kc, :], hT[:, kc, HALO + gi * 512:HALO + (gi + 1) * 512],
                         start=(kc == 0), stop=(kc == 7))
                P.act(z[:, :], ps3[:, :], AF.Silu)
                P.tt("pool", yT[:, 0 + h, cs], o_sb[:, cs], z[:, :], ALU.mult)
                yield
    def gen_D():
        for h in range(NH):
            wsD.get(h)
            P.dma("sp", o_sbD[:, :], K.ret_o[h, :, c0:c0 + HT], key="d_o")
            P.dma("sp", q_sb[:, :], K.ret_q[h, :, c0:c0 + HT], key="d_q")
            for gi in range(2):
                cs = slice(gi * 512, (gi + 1) * 512)
                ps = K.ps_tile()
                P.mm(ps[:, :], L.Srin[:, h, :], q_sb[:, cs])
                P.tt("dve", o_sbD[:, cs], o_sbD[:, cs], ps[:, :], ALU.add)
                yield
                sq = sqD[gi % 2]
                m = mt[gi % 2]
                P.act(sq[:, :], o_sbD[:, cs], AF.Square)
                ps1 = K.ps_tile()
                P.mm(ps1[:, :], C.ones[:, :], o_sbD[:, cs])
                ps2 = K.ps_tile()
                P.mm(ps2[:, :], C.ones[:, :], sq[:, :])
                yield
                P.ts("dve", m[:, :], ps1[:, :], 1.0 / 128.0, None, op0=ALU.mult)
                P.tt("pool", sq[:, :], m[:, :], m[:, :], ALU.mult)
                P.stt("dve", sq[:, :], ps2[:, :], 1.0 / 128.0, sq[:, :], ALU.mult, ALU.subtract)
                rsqrt(K, sq[:, :], sq[:, :], 1.0, EPS)
                P.tt("pool", o_sbD[:, cs], o_sbD[:, cs], m[:, :], ALU.subtract)
                P.stt("dve", o_sbD[:, cs], o_sbD[:, cs], K.VT[l][:, 117 + h:118 + h], sq[:, :], ALU.mult, ALU.mult)
                yield
                z = zsD[gi % 2]
                ps3 = K.ps_tile()
                for kc in range(8):
                    P.mm(ps3[:, :], wzD[h % 2][:, kc, :], hT[:, kc, HALO + gi * 512:HALO + (gi + 1) * 512],
                         start=(kc == 0), stop=(kc == 7))
                P.act(z[:, :], ps3[:, :], AF.Silu)
                P.tt("pool", yT[:, 12 + h, cs], o_sbD[:, cs], z[:, :], ALU.mult)
                yield
    run_rr([gen_A(), gen_D()])
    K.sb_cur = mark


def phase_c_merge(K, hT, hh, yT):
    P, sb, I, C, L = K.P, K.sb, K.I, K.C, K.L
    l = L.l
    HT = 1024
    mark = K.sb_cur
    merged = sb("m_merged", [128, 8, HT], BF16)
    wg = [sb(f"m_wg{i}", [128, 4, 8, 128], BF16) for i in range(2)]
    wb = [sb(f"m_wb{i}", [128, 4, 4, 128], BF16) for i in range(2)]
    sg = [sb(f"m_sg{i}", [128, 512], F32) for i in range(2)]
    acc = [sb(f"m_acc{i}", [128, 512], F32) for i in range(2)]
    tmp = [sb(f"m_tmp{i}", [128, 512], F32) for i in range(2)]
    brw = ("w_br_dn", "w_br_pool", "w_br_sg", "w_br_ret")
    n = 0

    def ld_m(buf, bi, dc):
        for br in range(4):
            load_w(K, wg[bi][:, br], I["w_in"][l], C_GATES + br * D + dc * 128, 128, key=f"m_wg{bi}_{br}")
            load_w(K, wb[bi][:, br], I[brw[br]][l], dc * 128, 128, key=f"m_wb{bi}_{br}")

    wsm = WStream([0, 1], [(lambda buf, bi, dc=dc: ld_m(buf, bi, dc)) for dc in range(8)], 1)
    wo = [sb(f"m_wo{i}", [128, 8, 128], BF16) for i in range(2)]
    wso = WStream(wo, [(lambda buf, bi, dc=dc: load_w(K, buf, I["w_out"][l], dc * 128, 128, key=f"m_wo{bi}"))
                       for dc in range(8)], 1)
    for dc in range(8):
        wsm.get(dc)
        for gi in range(2):
            cs = slice(gi * 512, (gi + 1) * 512)
            a = acc[gi % 2]
            for br in range(4):
                s_ = sg[n % 2]
                t_ = tmp[n % 2]
                n += 1
                ps = K.ps_tile()
                for kc in range(8):
                    P.mm(ps[:, :], wg[dc % 2][:, br, kc, :], hT[:, kc, HALO + gi * 512:HALO + (gi + 1) * 512],
                         start=(kc == 0), stop=(kc == 7))
                P.act(s_[:, :], ps[:, :], AF.Sigmoid)
                ps2 = K.ps_tile()
                for kc in range(4):
                    P.mm(ps2[:, :], wb[dc % 2][:, br, kc, :], yT[:, br * 4 + kc, cs], start=(kc == 0), stop=(kc == 3))
                if br == 0:
                    P.tt("dve", a[:, :], s_[:, :], ps2[:, :], ALU.mult)
                elif br < 3:
                    P.tt("dve", t_[:, :], s_[:, :], ps2[:, :], ALU.mult)
                    P.tt("dve", a[:, :], a[:, :], t_[:, :], ALU.add)
                else:
                    P.tt("dve", t_[:, :], s_[:, :], ps2[:, :], ALU.mult)
                    P.tt("pool", merged[:, dc, cs], a[:, :], t_[:, :], ALU.add)
    K.dump(f"merged{l}_{hh}", merged[:, :, :], [128, 8, HT], BF16)
    G1 = K.modT[l][:, 16:24]
    for dc in range(8):
        wo = {dc % 2: wso.get(dc)}
        for gi in range(2):
            cs = slice(gi * 512, (gi + 1) * 512)
            xs = K.xT[:, dc, HALO + hh * HT + gi * 512:HALO + hh * HT + (gi + 1) * 512]
            ps = K.ps_tile()
            for kc in range(8):
                P.mm(ps[:, :], wo[dc % 2][:, kc, :], merged[:, kc, cs], start=(kc == 0), stop=(kc == 7))
            P.stt("dve", xs, ps[:, :], G1[:, dc:dc + 1], xs, ALU.mult, ALU.add)
    K.sb_cur = mark


def mlp_half(K, l, hh, hT):
    P, sb, I, C = K.P, K.sb, K.I, K.C
    HT = 1024
    A2, B2, G2 = K.AB[l][:, 16:24], K.modT[l][:, 24:32], K.modT[l][:, 40:48]
    norm_h(K, A2, B2, HALO + hh * HT, HT, hT, HALO)
    mark = K.sb_cur
    uT = sb("f_uT", [128, 32, HT], BF16)
    w1b = [sb(f"f_w1{i}", [128, 8, 128], BF16) for i in range(4)]
    rl = [sb(f"f_rl{i}", [128, 512], BF16) for i in range(2)]
    n = 0
    ws1 = WStream(w1b, [(lambda buf, bi, fc=fc: load_w(K, buf, I["mlp_w1"][l], fc * 128, 128, key=f"f_w1{bi}"))
                        for fc in range(32)], 3)
    w2 = [sb(f"f_w2{i}", [128, 32, 128], BF16) for i in range(2)]
    ws2 = WStream(w2, [(lambda buf, bi, dc=dc: load_w(K, buf, I["mlp_w2"][l], dc * 128, 128, key=f"f_w2{bi}"))
                       for dc in range(8)], 1)
    for fc in range(32):
        w1 = {fc % 2: ws1.get(fc)}
        for gi in range(2):
            cs = slice(gi * 512, (gi + 1) * 512)
            ps = K.ps_tile()
            for kc in range(8):
                P.mm(ps[:, :], w1[fc % 2][:, kc, :], hT[:, kc, HALO + gi * 512:HALO + (gi + 1) * 512],
                     start=(kc == 0), stop=(kc == 7))
            r = rl[n % 2]
            n += 1
            P.act(r[:, :], ps[:, :], AF.Relu)
            P.tt("pool", uT[:, fc, cs], r[:, :], r[:, :], ALU.mult)
    for dc in range(8):
        w2 = {dc % 2: ws2.get(dc)}
        for gi in range(2):
            cs = slice(gi * 512, (gi + 1) * 512)
            xs = K.xT[:, dc, HALO + hh * HT + gi * 512:HALO + hh * HT + (gi + 1) * 512]
            ps = K.ps_tile()
            for kc in range(32):
                P.mm(ps[:, :], w2[dc % 2][:, kc, :], uT[:, kc, cs], start=(kc == 0), stop=(kc == 31))
            P.stt("dve", xs, ps[:, :], G2[:, dc:dc + 1], xs, ALU.mult, ALU.add)
    K.sb_cur = mark


def exchange_xhalo(K):
    P, sb, I, C = K.P, K.sb, K.I, K.C
    mark = K.sb_cur
    st = sb("xh_st", [128, 8, HALO], F32)
    P.copy("pool", st[:, :, :], K.xT[:, :, T:T + HALO])
    P.dma("sp", K.xh_src.rearrange("(fc p) n -> p fc n", p=128), st[:, :, :], key="xh_src")
    P.allgather(K.xh_all, K.xh_src, GROUPS, key="ag_xh")
    xin = sb("xh_in", [128, 3, 8, HALO], F32)
    src = K.xh_all.rearrange("(r fc p) n -> p r fc n", r=RPB, fc=8, p=128)
    sel = sb("xh_sel", [128, 4], F32)
    for r in range(3):
        P.dma("sp", xin[:, r], src[:, r], key=f"xh_in{r}")
        P.ts("dve", sel[:, r:r + 1], K.meta[:, 0:1], float(r + 1), None, op0=ALU.is_equal)
    P.ts("dve", K.xT[:, :, 0:HALO], xin[:, 0], sel[:, 0:1], None, op0=ALU.mult)
    for r in (1, 2):
        P.stt("dve", K.xT[:, :, 0:HALO], xin[:, r], sel[:, r:r + 1], K.xT[:, :, 0:HALO], ALU.mult, ALU.add)
    K.sb_cur = mark


def final_norm(K):
    P, sb, I, C = K.P, K.sb, K.I, K.C
    mark = K.sb_cur
    sq = [sb(f"fn_sq{i}", [128, 512], F32) for i in range(2)]
    rs = sb("fn_rs", [128, 512], F32)
    yT = sb("fn_y", [128, 8, 512], F32)
    ot = [sb(f"fn_o{i}", [128, D], F32) for i in range(2)]
    for gi in range(T // 512):
        c0 = HALO + gi * 512
        ps = K.ps_tile()
        for fc in range(8):
            q = sq[fc % 2]
            P.act(q[:, :], K.xT[:, fc, c0:c0 + 512], AF.Square)
            P.mm(ps[:, :], C.ones[:, :], q[:, :], start=(fc == 0), stop=(fc == 7))
        rsqrt(K, rs[:, :], ps[:, :], 1.0 / D, EPS)
        for fc in range(8):
            P.stt("dve", yT[:, fc, :], K.xT[:, fc, c0:c0 + 512], K.fgT[:, fc:fc + 1], rs[:, :], ALU.mult, ALU.mult)
        for t4 in range(4):
            tt = gi * 4 + t4
            o = ot[tt % 2]
            for half in range(2):
                ps2 = K.ps_tile()
                for q4 in range(4):
                    fc = half * 4 + q4
                    P.tr(ps2[:, q4 * 128:(q4 + 1) * 128], yT[:, fc, t4 * 128:(t4 + 1) * 128], C.ident[:])
                P.copy("act", o[:, half * 512:(half + 1) * 512], ps2[:, :])
            P.dma("sp", K.out[tt * 128:(tt + 1) * 128, :], o[:, :], key=f"out{tt % 2}")
    K.sb_cur = mark


def layer(K, l, stop_after):
    P, sb, I, C = K.P, K.sb, K.I, K.C
    mark = K.sb_cur
    K.sq = [sb(f"sq{i}", [128, 512], F32) for i in range(2)]
    L = Ctx()
    K.L = L
    L.l = l
    mark_c = K.sb_cur
    layer_setup_c(K, l)
    mark_a = K.sb_cur
    layer_setup(K, l)
    A1, B1 = K.AB[l][:, 0:8], K.modT[l][:, 0:8]
    HT = 1024
    for hh in range(2):
        m2 = K.sb_cur
        hT = sb("hT", [128, 8, HALO + HT], BF16)
        K.cs_half = [sb(f"cs_half{i}", [128, HT], F32) for i in range(2)]
        for i in range(2):
            P.dma("sp", K.cs_half[i][:, :], K.cs_dram[i, :, hh * HT:(hh + 1) * HT], key=f"cs_half{i}")
        if hh == 0:
            norm_h(K, A1, B1, 0, HALO, hT, 0)
        norm_h(K, A1, B1, HALO + hh * HT, HT, hT, HALO)
        K.dump(f"hT{l}_{hh}", hT[:, :, :], [128, 8, HALO + HT], BF16)
        if stop_after == f"L{l}h{hh}norm":
            return
        bg_unit(K, hT, hh)
        for h in range(NH):
            if stop_after == f"L{l}h{hh}ret{h}":
                ret_head_unit(K, hT, hh, h)
                return
            dn_head_unit(K, hT, hh, h)
            if stop_after == f"L{l}h{hh}dn{h}":
                return
        rwb = [sb(f"rw3b{i}", [128, 3, 8, 128], BF16) for i in range(2)]

        def ld_r(buf, bi, h_):
            for ch, c0 in enumerate((C_RQ, C_RK, C_RV)):
                load_w(K, buf[:, ch], I["w_in"][l], c0 + h_ * 128, 128, key=f"rw3b{bi}_{ch}")

        wsr = WStream(rwb, [(lambda buf, bi, h_=h_: ld_r(buf, bi, h_)) for h_ in range(NH)], 1)
        for h in range(NH):
            ret_head_unit(K, hT, hh, h, w3=wsr.get(h))
        K.sb_cur = m2
    if stop_after == f"L{l}A":
        return
    exchange_states(K)
    if stop_after == f"L{l}X":
        return
    K.sb_cur = mark_a
    for hh in range(2):
        m2 = K.sb_cur
        hT = sb("hTc", [128, 8, HALO + HT], BF16)
        if hh == 0:
            norm_h(K, A1, B1, 0, HALO, hT, 0, compute_rstd=False)
        norm_h(K, A1, B1, HALO + hh * HT, HT, hT, HALO, compute_rstd=False)
        yT = sb("yT", [128, 16, HT], BF16)
        phase_c_branches(K, hT, hh, yT)
        K.dump(f"yT{l}_{hh}", yT[:, :, :], [128, 16, HT], BF16)
        if stop_after == f"L{l}C{hh}br":
            return
        phase_c_merge(K, hT, hh, yT)
        if stop_after == f"L{l}C{hh}":
            K.dump(f"xmid{l}_{hh}", K.xT[:, :, :], [128, 8, HALO + T])
            return
        K.sb_cur = m2
    K.sb_cur = mark_c
    for hh in range(2):
        m2 = K.sb_cur
        hT = sb("hTm", [128, 8, HALO + HT], BF16)
        mlp_half(K, l, hh, hT)
        K.sb_cur = m2
    K.dump(f"xout{l}", K.xT[:, :, :], [128, 8, HALO + T])
    if l < DEPTH - 1:
        exchange_xhalo(K)
    K.sb_cur = mark


_CACHE = {}


def make_in_maps(inputs):
    x = np.ascontiguousarray(inputs["x"], dtype=np.float32)
    c = np.asarray(inputs["c"], dtype=np.float32)
    pos = np.asarray(inputs["positions"], dtype=np.int32)
    shared = {}
    for k in ("norm1_g", "norm2_g", "ada_w", "ada_b", "w_in", "dn_conv_w", "dn_a_log", "dn_dt_bias",
              "dn_norm_g", "pool_w", "pool_scale", "sg_ln_g", "sg_ln_b", "sg_w", "sg_b", "ret_gn_g",
              "w_br_dn", "w_br_pool", "w_br_sg", "w_br_ret", "w_out", "mlp_w1", "mlp_w2"):
        shared[k] = np.ascontiguousarray(inputs[k], dtype=np.float32)
    shared["final_g"] = np.ascontiguousarray(inputs["final_g"], dtype=np.float32).reshape(1, D)
    maps = []
    for core in range(NCORE):
        b, j = divmod(core, RPB)
        m = dict(shared)
        m["x"] = np.ascontiguousarray(x[b, j * T:(j + 1) * T])
        if j == 0:
            m["xh"] = np.zeros((HALO, D), np.float32)
        else:
            m["xh"] = np.ascontiguousarray(x[b, j * T - HALO:j * T])
        m["cT"] = np.ascontiguousarray(c[b].reshape(8, 128).T)
        m["pos"] = np.ascontiguousarray(pos[b, j * T:(j + 1) * T].reshape(1, T))
        meta = np.zeros((128, 8), np.float32)
        meta[:, 0] = j
        meta[:, 1] = j * T
        invf = (10000.0 ** (-np.arange(0, 128, 2, dtype=np.float32) / np.float32(128))).astype(np.float32)
        meta[:, 2] = np.concatenate([invf, invf])
        m["meta"] = meta
        maps.append(m)
    return maps


def kernel(**inputs):
    if "nc" not in _CACHE:
        _CACHE["nc"] = build()[0]
    nc = _CACHE["nc"]
    maps = make_in_maps(inputs)
    res = run_bass_kernel_spmd(nc, maps, core_ids=list(range(NCORE)))
    out = np.empty((NB, SEQ, D), np.float32)
    for core in range(NCORE):
        b, j = divmod(core, RPB)
        out[b, j * T:(j + 1) * T] = res.results[core]["out"]
    return out
```
